# Optimizing a Trainium2 kernel written in Bass

```python
import math
import jax, jax.numpy as jnp
from jax import lax
import numpy as np

D_MODEL = 2048
BATCH = 2
SEQ = 8192
DEPTH = 2

GRID_W = 64
CTX_LEN = 256
HEAD_DIM = 128
A_Q_HEADS = D_MODEL // HEAD_DIM
A_KV_HEADS = A_Q_HEADS // 4
A_GROUP = A_Q_HEADS // A_KV_HEADS
WINDOW = 128
BLOCK = 128
B_QK_DIM = HEAD_DIM
B_V_DIM = 2 * B_QK_DIM
B_HEADS = D_MODEL // B_V_DIM
A_WIDTH = A_Q_HEADS * HEAD_DIM
A_KV_WIDTH = A_KV_HEADS * HEAD_DIM
B_QK_WIDTH = B_HEADS * 2 * B_QK_DIM
B_WIDTH = B_HEADS * B_V_DIM
KV_SPLITS = (A_KV_WIDTH, A_KV_WIDTH, B_QK_WIDTH, B_WIDTH)
Q_SPLITS = (A_WIDTH, A_WIDTH, B_QK_WIDTH, B_WIDTH, D_MODEL, D_MODEL)
KV_COLS = sum(KV_SPLITS)
Q_COLS = sum(Q_SPLITS)
IN_COLS = KV_COLS + Q_COLS
ROPE_THETA = 10000.0
EPS = 1e-6
NEG = -1e30
ADA_STD = 0.5

kernel_name = "hybrid_gated_window_gqa_diff_attn_dit"


def rmsnorm(x, g):
    xf = x.astype(jnp.float32)
    y = xf * lax.rsqrt(jnp.mean(xf * xf, axis=-1, keepdims=True) + EPS)
    return (y * g.astype(jnp.float32)).astype(x.dtype)


def split_cols(p, sizes):
    idx = np.cumsum(sizes)[:-1].tolist()
    return jnp.split(p, idx, axis=-1)


def axial_rope_tables(n_lat):
    rows = n_lat // GRID_W
    r = jnp.repeat(jnp.arange(rows, dtype=jnp.float32), GRID_W)
    col = jnp.tile(jnp.arange(GRID_W, dtype=jnp.float32), rows)
    n_freq = HEAD_DIM // 4
    inv = ROPE_THETA ** (-jnp.arange(n_freq, dtype=jnp.float32) / n_freq)
    ang = jnp.concatenate([r[:, None] * inv, col[:, None] * inv], axis=-1)
    return jnp.cos(ang), jnp.sin(ang)


def apply_rope(x, cos, sin):
    half = x.shape[-1] // 2
    x1, x2 = x[..., :half], x[..., half:]
    c = cos[None, :, None, :].astype(x.dtype)
    s = sin[None, :, None, :].astype(x.dtype)
    return jnp.concatenate([x1 * c - x2 * s, x2 * c + x1 * s], axis=-1)


def adaln(cvec, w, b):
    m = jax.nn.silu(cvec) @ w + b
    return jnp.split(m, 3, axis=-1)


def window_gqa_latent(q, k, v, k_ctx, v_ctx, sink):
    B, S = q.shape[0], q.shape[1]
    C = k_ctx.shape[1]
    nb = S // BLOCK
    scale = HEAD_DIM ** -0.5
    qb = q.reshape(B, nb, BLOCK, A_KV_HEADS, A_GROUP, HEAD_DIM) * scale
    pad = ((0, 0), (BLOCK, BLOCK), (0, 0), (0, 0))
    kb = jnp.pad(k, pad).reshape(B, nb + 2, BLOCK, A_KV_HEADS, HEAD_DIM)
    vb = jnp.pad(v, pad).reshape(B, nb + 2, BLOCK, A_KV_HEADS, HEAD_DIM)
    kw = jnp.concatenate([kb[:, :-2], kb[:, 1:-1], kb[:, 2:]], axis=2)
    vw = jnp.concatenate([vb[:, :-2], vb[:, 1:-1], vb[:, 2:]], axis=2)
    s_loc = jnp.einsum('bnqhgd,bnkhd->bnhgqk', qb, kw).astype(jnp.float32)
    blk = jnp.arange(nb)[:, None, None]
    qpos = blk * BLOCK + jnp.arange(BLOCK)[None, :, None]
    kpos = (blk - 1) * BLOCK + jnp.arange(3 * BLOCK)[None, None, :]
    valid = (kpos >= 0) & (kpos < S) & (jnp.abs(qpos - kpos) <= WINDOW)
    s_loc = jnp.where(valid[None, :, None, None], s_loc, NEG)
    s_ctx = jnp.einsum('bnqhgd,bchd->bnhgqc', qb, k_ctx).astype(jnp.float32)
    s_sink = jnp.broadcast_to(sink.astype(jnp.float32).reshape(1, 1, A_KV_HEADS, A_GROUP, 1, 1),
                              s_ctx.shape[:-1] + (1,))
    p = jax.nn.softmax(jnp.concatenate([s_loc, s_ctx, s_sink], axis=-1), axis=-1)
    p_loc = p[..., :3 * BLOCK].astype(v.dtype)
    p_ctx = p[..., 3 * BLOCK:3 * BLOCK + C].astype(v.dtype)
    o = (jnp.einsum('bnhgqk,bnkhd->bnqhgd', p_loc, vw)
         + jnp.einsum('bnhgqc,bchd->bnqhgd', p_ctx, v_ctx))
    return o.reshape(B, S, A_WIDTH)


def window_gqa_context(q, k, v, sink):
    B, C = q.shape[0], q.shape[1]
    qg = q.reshape(B, C, A_KV_HEADS, A_GROUP, HEAD_DIM) * (HEAD_DIM ** -0.5)
    s = jnp.einsum('bqhgd,bkhd->bhgqk', qg, k).astype(jnp.float32)
    s_sink = jnp.broadcast_to(sink.astype(jnp.float32).reshape(1, A_KV_HEADS, A_GROUP, 1, 1),
                              s.shape[:-1] + (1,))
    p = jax.nn.softmax(jnp.concatenate([s, s_sink], axis=-1), axis=-1)[..., :C].astype(v.dtype)
    o = jnp.einsum('bhgqk,bkhd->bqhgd', p, v)
    return o.reshape(B, C, A_WIDTH)


def diff_attn_latent(q, k_all, v_all, lam):
    B, S = q.shape[0], q.shape[1]
    nb = S // BLOCK
    qb = jnp.moveaxis(q.reshape(B, nb, BLOCK, B_HEADS, 2, B_QK_DIM) * (B_QK_DIM ** -0.5), 1, 0)

    def one_block(qblk):
        s = jnp.einsum('bqhmd,bkhmd->bhmqk', qblk, k_all).astype(jnp.float32)
        p = jax.nn.softmax(s, axis=-1)
        pd = (p[:, :, 0] - lam * p[:, :, 1]).astype(v_all.dtype)
        return jnp.einsum('bhqk,bkhe->bqhe', pd, v_all)

    o = lax.map(one_block, qb)
    return jnp.moveaxis(o, 0, 1).reshape(B, S, B_HEADS, B_V_DIM)


def diff_attn_context(q, k, v, lam):
    s = jnp.einsum('bqhmd,bkhmd->bhmqk', q * (B_QK_DIM ** -0.5), k).astype(jnp.float32)
    p = jax.nn.softmax(s, axis=-1)
    pd = (p[:, :, 0] - lam * p[:, :, 1]).astype(v.dtype)
    return jnp.einsum('bhqk,bkhe->bqhe', pd, v)


def diff_output(o, g_sub, lam_init):
    B, L = o.shape[0], o.shape[1]
    return (rmsnorm(o, g_sub) * (1.0 - lam_init)).reshape(B, L, B_WIDTH)


def branch_merge(o_a, z_a, o_b, z_b, g_a, g_b, wpa, wpb, wo):
    y_a = (o_a * jax.nn.silu(z_a)) @ wpa
    y_b = (o_b * jax.nn.silu(z_b)) @ wpb
    return (jax.nn.sigmoid(g_a) * y_a + jax.nn.sigmoid(g_b) * y_b) @ wo


def setup_inputs(seed: int = 0) -> dict:
    key = jax.random.key(seed)
    ks = jax.random.split(key, 16)
    f32 = jnp.float32
    D = D_MODEL
    nrm = lambda k, shape, s: jax.random.normal(k, shape, f32) * s
    return {
        "x": nrm(ks[0], (BATCH, SEQ, D), 1.0),
        "c": nrm(ks[1], (BATCH, D), 1.0),
        "ctx": nrm(ks[2], (BATCH, CTX_LEN, D), 1.0),
        "c_ctx": nrm(ks[3], (D,), 1.0),
        "w_ada": nrm(ks[4], (DEPTH, D, 3 * D), ADA_STD * D ** -0.5),
        "b_ada": nrm(ks[5], (DEPTH, 3 * D), 0.01),
        "g_pre": 1.0 + nrm(ks[6], (DEPTH, D), 0.02),
        "g_post": 1.0 + nrm(ks[7], (DEPTH, D), 0.02),
        "w_in": nrm(ks[8], (DEPTH, D, IN_COLS), D ** -0.5),
        "sink": nrm(ks[9], (DEPTH, A_Q_HEADS), 1.0),
        "lam_qk": nrm(ks[10], (DEPTH, 4, B_QK_DIM), 0.1),
        "g_subln": 1.0 + nrm(ks[11], (DEPTH, B_V_DIM), 0.02),
        "w_proj_a": nrm(ks[12], (DEPTH, A_WIDTH, D), A_WIDTH ** -0.5),
        "w_proj_b": nrm(ks[13], (DEPTH, B_WIDTH, D), B_WIDTH ** -0.5),
        "w_out": nrm(ks[14], (DEPTH, D, D), D ** -0.5),
    }


def reference(x, c, ctx, c_ctx, w_ada, b_ada, g_pre, g_post, w_in, sink, lam_qk, g_subln,
              w_proj_a, w_proj_b, w_out):
    B, S = x.shape[0], x.shape[1]
    C = ctx.shape[1]
    cos, sin = axial_rope_tables(S)
    for l in range(DEPTH):
        last = l == DEPTH - 1
        lam_init = 0.8 - 0.6 * math.exp(-0.3 * l)
        lq1, lk1, lq2, lk2 = [t.astype(jnp.float32) for t in lam_qk[l]]
        lam = jnp.exp(jnp.sum(lq1 * lk1)) - jnp.exp(jnp.sum(lq2 * lk2)) + lam_init

        sh_x, sc_x, gt_x = adaln(c, w_ada[l], b_ada[l])
        sh_c, sc_c, gt_c = adaln(c_ctx, w_ada[l], b_ada[l])
        hx = rmsnorm(x, g_pre[l]) * (1.0 + sc_x[:, None]) + sh_x[:, None]
        hc = rmsnorm(ctx, g_pre[l]) * (1.0 + sc_c) + sh_c

        px = hx @ w_in[l]
        k_a, v_a, k_b, v_b = split_cols(px[..., :KV_COLS], KV_SPLITS)
        q_a, z_a, q_b, z_b, g_a, g_b = split_cols(px[..., KV_COLS:], Q_SPLITS)
        q_a = apply_rope(q_a.reshape(B, S, A_Q_HEADS, HEAD_DIM), cos, sin)
        k_a = apply_rope(k_a.reshape(B, S, A_KV_HEADS, HEAD_DIM), cos, sin)
        v_a = v_a.reshape(B, S, A_KV_HEADS, HEAD_DIM)
        q_b = apply_rope(q_b.reshape(B, S, 2 * B_HEADS, B_QK_DIM), cos, sin).reshape(B, S, B_HEADS, 2, B_QK_DIM)
        k_b = apply_rope(k_b.reshape(B, S, 2 * B_HEADS, B_QK_DIM), cos, sin).reshape(B, S, B_HEADS, 2, B_QK_DIM)
        v_b = v_b.reshape(B, S, B_HEADS, B_V_DIM)

        pc_kv = hc @ w_in[l][:, :KV_COLS]
        kc_a, vc_a, kc_b, vc_b = split_cols(pc_kv, KV_SPLITS)
        kc_a = kc_a.reshape(B, C, A_KV_HEADS, HEAD_DIM)
        vc_a = vc_a.reshape(B, C, A_KV_HEADS, HEAD_DIM)
        kc_b = kc_b.reshape(B, C, B_HEADS, 2, B_QK_DIM)
        vc_b = vc_b.reshape(B, C, B_HEADS, B_V_DIM)

        o_a = window_gqa_latent(q_a, k_a, v_a, kc_a, vc_a, sink[l])
        k_all = jnp.concatenate([kc_b, k_b], axis=1)
        v_all = jnp.concatenate([vc_b, v_b], axis=1)
        o_b = diff_output(diff_attn_latent(q_b, k_all, v_all, lam), g_subln[l], lam_init)
        out_x = branch_merge(o_a, z_a, o_b, z_b, g_a, g_b, w_proj_a[l], w_proj_b[l], w_out[l])

        if not last:
            pc_q = hc @ w_in[l][:, KV_COLS:]
            qc_a, zc_a, qc_b, zc_b, gc_a, gc_b = split_cols(pc_q, Q_SPLITS)
            oc_a = window_gqa_context(qc_a.reshape(B, C, A_Q_HEADS, HEAD_DIM), kc_a, vc_a, sink[l])
            oc_b = diff_output(diff_attn_context(qc_b.reshape(B, C, B_HEADS, 2, B_QK_DIM), kc_b, vc_b, lam),
                               g_subln[l], lam_init)
            out_c = branch_merge(oc_a, zc_a, oc_b, zc_b, gc_a, gc_b, w_proj_a[l], w_proj_b[l], w_out[l])
            ctx = ctx + gt_c * rmsnorm(out_c, g_post[l])

        x = x + gt_x[:, None] * rmsnorm(out_x, g_post[l])
    return x
```

```python
import math
from contextlib import ExitStack

import numpy as np
import concourse.bass as bass
import concourse.mybir as mybir
from concourse.bass_utils import run_bass_kernel_spmd

F32 = mybir.dt.float32
BF16 = mybir.dt.bfloat16
AF = mybir.ActivationFunctionType
ALU = mybir.AluOpType
AX = mybir.AxisListType

D = 2048
SEQ = 8192
CTX = 256
TL = 2048
NT = TL + CTX
NTILE = NT // 128
DEPTH = 2
IN_COLS = 17408
EPS = 1e-6
SCALE = 128 ** -0.5
GROWS = 5120
R_KTA, R_KTB, R_VA, R_VB = 0, 512, 2560, 3072

CENGS = ("tensor", "vector", "scalar", "gpsimd")
ENGS = CENGS + ("sync",)
GCH = 256
NGCH = GROWS // GCH


def gat_rows(gat, r, row0, n):
    assert row0 // GCH == (row0 + n - 1) // GCH
    return gat[row0 // GCH, r, row0 % GCH:row0 % GCH + n, :]


class Buf:
    __slots__ = ("name", "lw", "rd", "dsem", "dtotal", "dsynced", "dbase", "slot", "_stored")

    def __init__(self, name):
        self.name = name
        self.lw = None
        self.rd = {}
        self.dsem = None
        self.dtotal = 0
        self.dsynced = {}
        self.dbase = 0
        self.slot = None


class Plan:
    nsem = 0

    def __init__(self, K, new_sem):
        self.nc = K.nc
        self.new_sem = new_sem
        self.esem = K.esem
        self.ecount = K.ecount
        self.waited = K.waited
        self.ops = {e: [] for e in ENGS}
        self.dbufs = []
        self.noinc_pending = {e: False for e in CENGS}

    def _wait_eng(self, E, waits, F, seq):
        w = self.waited[E]
        if w.get(F, 0) >= seq:
            return
        w[F] = seq
        waits.append((self.esem[F], seq))

    def _wait_dma(self, E, waits, b):
        if b.dtotal > b.dsynced.get(E, b.dbase):
            b.dsynced[E] = b.dtotal
            waits.append((b.dsem, b.dtotal))

    def _deps(self, E, reads, writes, dma_load=False):
        waits = []
        for b in reads:
            if b.lw is not None:
                self._wait_eng(E, waits, b.lw[0], b.lw[1])
            self._wait_dma(E, waits, b)
        for b in writes:
            if b.lw is not None and b.lw[0] != E:
                self._wait_eng(E, waits, b.lw[0], b.lw[1])
            for F, s in b.rd.items():
                if F != E:
                    self._wait_eng(E, waits, F, s)
            if not dma_load:
                self._wait_dma(E, waits, b)
        return waits

    def op(self, E, fn, reads=(), writes=(), inc=True):
        waits = self._deps(E, reads, writes)
        if inc:
            self.ecount[E] += 1
            seq = self.ecount[E]
        else:
            seq = self.ecount[E] + 1
        self.noinc_pending[E] = not inc
        for b in reads:
            b.rd[E] = seq
        for b in writes:
            b.lw = (E, seq)
            b.rd = {}
        self.ops[E].append((waits, fn, (self.esem[E], 1) if inc else None))

    def dma(self, Q, fn, buf, load):
        if buf.dsem is None:
            buf.slot = self.new_sem(buf.name)
            buf.dsem = buf.slot[0]
            buf.dbase = buf.dtotal = buf.slot[1]
            self.dbufs.append(buf)
        waits = self._deps(Q, () if load else (buf,), (buf,) if load else (), dma_load=load)
        if load:
            assert not getattr(buf, "_stored", False)
            buf.lw = None
            buf.rd = {}
        else:
            buf._stored = True
        buf.dtotal += 16
        self.ops[Q].append((waits, fn, (buf.dsem, 16)))

    def finish(self, Q="sync"):
        assert not any(self.noinc_pending.values()), self.noinc_pending
        waits = []
        for b in self.dbufs:
            self._wait_dma(Q, waits, b)
            b.slot[1] = b.dtotal
        if waits:
            self.ops[Q].append((waits, None, None))

    def replay(self, block):
        def mk(E):
            ops = self.ops[E]

            def body(eng):
                for waits, fn, inc in ops:
                    for sem, val in waits:
                        eng.wait_ge(sem, val)
                    if fn is not None:
                        ins = fn(eng)
                        if inc is not None:
                            ins.then_inc(inc[0], inc[1])
            return body
        for E in ENGS:
            if self.ops[E]:
                getattr(block, E)(mk(E))


class Ctx:
    pass


def run_phase(K, body):
    nslot = [0]

    def new_slot(name):
        sl = K.dslots[nslot[0]]
        nslot[0] += 1
        return sl
    with ExitStack() as st:
        K.pid += 1
        P = Plan(K, new_slot)
        cnt = [0]

        def sb(name, shape, dt):
            cnt[0] += 1
            t = st.enter_context(K.nc.sbuf_tensor(f"{name}_{K.pid}_{cnt[0]}", list(shape), dt))
            return t, Buf(name)

        def ps(name, shape, dt):
            cnt[0] += 1
            t = st.enter_context(K.nc.psum_tensor(f"{name}_{K.pid}_{cnt[0]}", list(shape), dt))
            return t, Buf(name)
        body(P, sb, ps)
        P.finish()
        K.marks.append((getattr(body, "__qualname__", "?"), dict(K.ecount)))
        with K.nc.Block() as block:
            P.replay(block)


def use_layer_sems(K, l):
    K.esem = K.esets[l]
    K.ecount = {e: 0 for e in CENGS}
    K.waited = {e: {} for e in ENGS}


def o_mm(out, lhsT, rhs, start, stop):
    return lambda e: e.matmul(out, lhsT=lhsT, rhs=rhs, start=start, stop=stop)


def o_tr(out, in_, ident):
    return lambda e: e.transpose(out, in_, ident)


def o_act(out, in_, func, scale=None, bias=None, accum_out=None):
    kw = {}
    if scale is not None:
        kw["scale"] = scale
    if bias is not None:
        kw["bias"] = bias
    if accum_out is not None:
        kw["accum_out"] = accum_out
    return lambda e: e.activation(out=out, in_=in_, func=func, **kw)


def o_tt(out, in0, in1, op):
    return lambda e: e.tensor_tensor(out=out, in0=in0, in1=in1, op=op)


def o_ts(out, in0, s1, op0, s2=None, op1=None):
    if op1 is None:
        return lambda e: e.tensor_scalar(out=out, in0=in0, scalar1=s1, scalar2=None, op0=op0)
    return lambda e: e.tensor_scalar(out=out, in0=in0, scalar1=s1, scalar2=s2, op0=op0, op1=op1)


def o_stt(out, in0, scalar, in1, op0, op1):
    return lambda e: e.scalar_tensor_tensor(out=out, in0=in0, scalar=scalar, in1=in1, op0=op0, op1=op1)


def o_copy(out, in_):
    return lambda e: e.tensor_copy(out=out, in_=in_)


def o_acopy(out, in_):
    return lambda e: e.copy(out=out, in_=in_)


def o_memset(ap, v):
    return lambda e: e.memset(ap, v)


def o_recip(out, in_):
    return lambda e: e.reciprocal(out=out, in_=in_)


def o_rsum(out, in_):
    return lambda e: e.reduce_sum(out=out, in_=in_, axis=AX.X)


def o_dma(out, in_, slow=False):
    if slow:
        return lambda e: e.dma_start(out=out, in_=in_, allow_slow_non_contiguous=True)
    return lambda e: e.dma_start(out=out, in_=in_)


def o_gather(gth, gat, i):
    return lambda e: e.collective_compute("AllGather", ALU.bypass, replica_groups=[[0, 1, 2, 3], [4, 5, 6, 7]],
                                          ins=[gth[i * GCH:(i + 1) * GCH, :]],
                                          outs=[gat[i].rearrange("r g c -> (r g) c")])


def fm_vec(ap1d):
    return ap1d.rearrange("(k p) -> p k", p=128)


def make_ident(P, sb, dt):
    idf, IDF = sb("identf", [128, 128], F32)
    P.op("gpsimd", o_memset(idf[:], 1.0), writes=(IDF,))
    P.op("gpsimd", lambda e: e.affine_select(out=idf[:], in_=idf[:], pattern=[[-1, 128]],
                                              compare_op=ALU.is_equal, fill=0.0, base=0,
                                              channel_multiplier=1), reads=(IDF,), writes=(IDF,))
    if dt == F32:
        return idf, IDF
    idb, IDB = sb("identb", [128, 128], BF16)
    P.op("vector", o_copy(idb[:], idf[:]), reads=(IDF,), writes=(IDB,))
    return idb, IDB


def phase_adaln(K, l):
    def body(P, sb, ps):
        ccT, CCT = sb("ccT", [128, 2, 16], F32)
        for r in range(2):
            P.dma("sync", o_dma(ccT[:, r, :], fm_vec(K.cc[r]), slow=True), CCT, True)
        P.op("scalar", o_act(ccT[:], ccT[:], AF.Silu), reads=(CCT,), writes=(CCT,))
        brow, BROW = sb("brow", [2, 3 * D], F32)
        P.dma("sync", o_dma(brow[:], K.b_ada[l].partition_broadcast(2)), BROW, True)
        mrow, MROW = sb("mrow", [2, 3 * D], F32)
        wv = K.w_ada[l].rearrange("(k p) c -> p k c", p=128)
        wts = [sb("wada", [128, 16, 512], F32) for _ in range(2)]
        pss = [ps("pada", [128, 512], F32) for _ in range(2)]
        for ct in range(12):
            wt, WT = wts[ct % 2]
            pt, PT = pss[ct % 2]
            cs = slice(ct * 512, (ct + 1) * 512)
            P.dma("sync", o_dma(wt[:], wv[:, :, cs]), WT, True)
            for kc in range(16):
                P.op("tensor", o_mm(pt[0:2, :], ccT[:, :, kc], wt[:, kc, :], kc == 0, kc == 15),
                     reads=(CCT, WT), writes=(PT,), inc=(kc == 15))
            P.op("vector", o_tt(mrow[:, cs], pt[0:2, :], brow[:, cs], ALU.add),
                 reads=(PT, BROW), writes=(MROW,))
        P.dma("sync", o_dma(K.mod[l], mrow[:]), MROW, False)
    run_phase(K, body)


def phase_rope(K):
    PI = math.pi

    def body(P, sb, ps):
        pos, POS = sb("pos", [64, TL], F32)
        fidx, FIDX = sb("fidx", [64, 1], F32)
        rb, RB = sb("rowb", [64, 1], F32)
        P.dma("sync", o_dma(rb[:], K.rowbase), RB, True)
        P.op("gpsimd", lambda e: e.iota(pos[0:32, :], pattern=[[1, TL // 64], [0, 64]], base=0, channel_multiplier=0,
                                        allow_small_or_imprecise_dtypes=True), writes=(POS,))
        P.op("gpsimd", lambda e: e.iota(pos[32:64, :], pattern=[[0, TL // 64], [1, 64]], base=0, channel_multiplier=0,
                                        allow_small_or_imprecise_dtypes=True), writes=(POS,))
        P.op("gpsimd", lambda e: e.iota(fidx[0:32, :], pattern=[[0, 1]], base=0, channel_multiplier=1,
                                        allow_small_or_imprecise_dtypes=True), writes=(FIDX,))
        P.op("gpsimd", lambda e: e.iota(fidx[32:64, :], pattern=[[0, 1]], base=0, channel_multiplier=1,
                                        allow_small_or_imprecise_dtypes=True), writes=(FIDX,))
        P.op("scalar", o_act(fidx[:], fidx[:], AF.Exp, scale=-math.log(10000.0) / 32.0), reads=(FIDX,), writes=(FIDX,))
        ang, ANG = sb("ang", [64, TL], F32)
        P.op("vector", o_ts(ang[:], pos[:], rb[:, 0:1], ALU.add, fidx[:, 0:1], ALU.mult),
             reads=(POS, RB, FIDX), writes=(ANG,))
        kf, KF = sb("kf", [64, TL], F32)
        ki, KI = sb("ki", [64, TL], mybir.dt.int32)
        red, RED = sb("red", [64, TL], F32)
        msk, MSK = sb("msk", [64, TL], F32)
        outc, OUTC = sb("outc", [64, TL], F32)
        outs, OUTS = sb("outs", [64, TL], F32)
        outn, OUTN = sb("outn", [64, TL], F32)

        def reduce_sin(shift, out, OUT):
            P.op("vector", o_ts(kf[:], ang[:], shift, ALU.add, 1.0 / (2 * PI), ALU.mult), reads=(ANG,), writes=(KF,))
            P.op("vector", o_copy(ki[:], kf[:]), reads=(KF,), writes=(KI,))
            P.op("vector", o_copy(kf[:], ki[:]), reads=(KI,), writes=(KF,))
            P.op("vector", o_stt(red[:], kf[:], -2 * PI, ang[:], ALU.mult, ALU.add), reads=(KF, ANG), writes=(RED,))
            if shift != 0.0:
                P.op("vector", o_ts(red[:], red[:], shift, ALU.add), reads=(RED,), writes=(RED,))
            P.op("vector", o_ts(msk[:], red[:], PI, ALU.is_gt), reads=(RED,), writes=(MSK,))
            P.op("vector", o_stt(red[:], msk[:], -2 * PI, red[:], ALU.mult, ALU.add), reads=(MSK, RED), writes=(RED,))
            P.op("vector", o_ts(msk[:], red[:], -PI, ALU.is_lt), reads=(RED,), writes=(MSK,))
            P.op("vector", o_stt(red[:], msk[:], 2 * PI, red[:], ALU.mult, ALU.add), reads=(MSK, RED), writes=(RED,))
            P.op("scalar", o_act(out[:], red[:], AF.Sin), reads=(RED,), writes=(OUT,))
        reduce_sin(0.0, outs, OUTS)
        reduce_sin(PI / 2, outc, OUTC)
        P.op("vector", o_ts(outn[:], outs[:], -1.0, ALU.mult), reads=(OUTS,), writes=(OUTN,))
        P.dma("sync", o_dma(K.ropeC[0:64, :], outc[:]), OUTC, False)
        P.dma("sync", o_dma(K.ropeC[64:128, :], outc[:]), OUTC, False)
        P.dma("sync", o_dma(K.ropeS[0:64, :], outn[:]), OUTN, False)
        P.dma("sync", o_dma(K.ropeS[64:128, :], outs[:]), OUTS, False)
    run_phase(K, body)


FAMS = [("k_a", 0, 512, "rope"), ("v_a", 512, 512, "v"), ("k_b", 1024, 2048, "rope"),
        ("v_b", 3072, 2048, "v"), ("q_a", 5120, 2048, "rope"), ("z_a", 7168, 2048, "z"),
        ("q_b", 9216, 2048, "rope"), ("z_b", 11264, 2048, "z"), ("g_a", 13312, 2048, "g"),
        ("g_b", 15360, 2048, "g")]


def phase_A(K, l, last, xsrc, fams=None, gather=False):
    T = K.T[l]

    def body(P, sb, ps):
        hxT, _ = sb("hxT", [128, 16, NT], BF16)
        HXA = [Buf(f"hxTa{i}") for i in range(5)]
        HXV = [Buf(f"hxTv{i}") for i in range(5)]
        idf, IDF = make_ident(P, sb, F32)
        xts = [sb("xt", [128, D], F32) for _ in range(2)]
        for t in range(2):
            P.dma("sync", o_dma(xts[t][0][:], xsrc[t * 128:(t + 1) * 128, :]), xts[t][1], True)
        cst, CST = sb("cst", [128, 5, 16], F32)
        P.dma("sync", o_dma(cst[:, 0, :], fm_vec(K.g_pre[l]), slow=True), CST, True)
        for r in range(2):
            P.dma("sync", o_dma(cst[:, 1 + 2 * r, :], fm_vec(K.mod[l][r, D:2 * D]), slow=True), CST, True)
            P.dma("sync", o_dma(cst[:, 2 + 2 * r, :], fm_vec(K.mod[l][r, 0:D]), slow=True), CST, True)
        for r in range(2):
            P.op("vector", o_stt(cst[:, 1 + 2 * r, :], cst[:, 1 + 2 * r, :], 1.0, cst[:, 0, :], ALU.add, ALU.mult),
                 reads=(CST,), writes=(CST,))
        ropeC, ROPEC = sb("ropeC", [128, TL], F32)
        ropeS, ROPES = sb("ropeS", [128, TL], F32)
        junk, JUNK = sb("junk", [128, D], BF16)
        stat, STAT = sb("stat", [128, 3, NTILE], F32)
        ptr = [ps("ptr", [128, 512], F32) for _ in range(4)]
        nb = 0
        for t in range(NTILE):
            xt, XT = xts[t % 2]
            r = 0 if t < 16 else 1
            if t >= 2:
                P.dma("sync", o_dma(xt[:], xsrc[t * 128:(t + 1) * 128, :]), XT, True)
            if t == 1:
                P.dma("sync", o_dma(ropeC[:], K.ropeC), ROPEC, True)
                P.dma("sync", o_dma(ropeS[:], K.ropeS), ROPES, True)
            P.op("scalar", o_act(junk[:], xt[:], AF.Square, accum_out=stat[:, 0, t:t + 1]),
                 reads=(XT,), writes=(JUNK, STAT))
            P.op("vector", o_ts(stat[:, 1, t:t + 1], stat[:, 0, t:t + 1], 1.0 / D, ALU.mult, EPS, ALU.add),
                 reads=(STAT,), writes=(STAT,))
            P.op("scalar", o_act(stat[:, 1, t:t + 1], stat[:, 1, t:t + 1], AF.Ln), reads=(STAT,), writes=(STAT,))
            P.op("scalar", o_act(stat[:, 2, t:t + 1], stat[:, 1, t:t + 1], AF.Exp, scale=-0.5),
                 reads=(STAT,), writes=(STAT,))
            P.op("vector", o_ts(xt[:], xt[:], stat[:, 2, t:t + 1], ALU.mult), reads=(XT, STAT), writes=(XT,))
            for kq in range(4):
                banks = (ptr[nb % 4], ptr[(nb + 1) % 4])
                nb += 2
                for j in range(4):
                    kc = kq * 4 + j
                    pt, PT = banks[j % 2]
                    P.op("tensor", o_tr(pt[:, (j // 2) * 128:(j // 2 + 1) * 128], xt[:, kc * 128:(kc + 1) * 128], idf[:]),
                         reads=(XT, IDF), writes=(PT,), inc=(j >= 2))
                for j in range(4):
                    kc = kq * 4 + j
                    pt, PT = banks[j % 2]
                    src = pt[:, (j // 2) * 128:(j // 2 + 1) * 128]
                    if j % 2 == 0:
                        P.op("scalar", o_act(hxT[:, kc, t * 128:(t + 1) * 128], src,
                                             AF.Identity, scale=cst[:, 1 + 2 * r, kc:kc + 1],
                                             bias=cst[:, 2 + 2 * r, kc:kc + 1]),
                             reads=(PT, CST), writes=(HXA[t // 4],))
                    else:
                        P.op("vector", o_ts(hxT[:, kc, t * 128:(t + 1) * 128], src,
                                            cst[:, 1 + 2 * r, kc:kc + 1], ALU.mult,
                                            cst[:, 2 + 2 * r, kc:kc + 1], ALU.add),
                             reads=(PT, CST), writes=(HXV[t // 4],))
        wv = K.w_in[l].rearrange("(k p) c -> p k c", p=128)
        wts = [sb("win", [128, 16, 512], BF16) for _ in range(3)]
        pacc = [ps("pacc", [128, 512], F32) for _ in range(4)]
        stg = [sb("stg", [128, 512], BF16) for _ in range(4)]
        tm1 = [sb("tm1", [128, 512], F32) for _ in range(2)]
        tm2 = [sb("tm2", [128, 512], F32) for _ in range(2)]
        cnt = {"w": 0, "p": 0, "s": 0, "t": 0}
        tts = [(0, 512), (512, 512), (1024, 512), (1536, 512), (2048, 256)]

        def fm_dest(name, c, t0, n):
            if name in ("k_a", "k_b"):
                if t0 < TL:
                    r0 = (R_KTA if name == "k_a" else R_KTB) + c * 128
                    return T["gth"][r0:r0 + 128, t0:t0 + n]
                return T["kvc_" + name][c, :, t0 - TL:t0 - TL + n]
            tn = {"q_a": "qt_a", "q_b": "qt_b", "g_a": "sg_a", "g_b": "sg_b"}[name]
            return T[tn][c, :, t0:t0 + n]

        def tm_dest(name, g, t):
            if name in ("v_a", "v_b"):
                if t < 16:
                    if name == "v_a":
                        return bass.AP(T["gth"].tensor, R_VA * 2048 + t * 128 * 512, [[512, 128], [1, 512]])
                    return T["gth"][R_VB + t * 128:R_VB + (t + 1) * 128, g * 512:(g + 1) * 512]
                return T["kvc_" + name][(t - 16) * 128:(t - 15) * 128, g * 512:(g + 1) * 512]
            return T[name][t * 128:(t + 1) * 128, g * 512:(g + 1) * 512]

        groups = [(name, c0, kind, g) for (name, c0, ncols, kind) in FAMS
                  if fams is None or name in fams for g in range(ncols // 512)]

        def wload(i):
            if i < len(groups):
                name_, c0_, _, g_ = groups[i]
                wt_, WT_ = wts[i % 3]
                P.dma("gpsimd", o_dma(wt_[:], wv[:, :, c0_ + g_ * 512:c0_ + (g_ + 1) * 512]), WT_, True)
        wload(0)
        wload(1)
        kv_done = False
        for gi, (name, c0, kind, g) in enumerate(groups):
            if True:
                qpart = c0 >= 5120
                if qpart and not kv_done:
                    kv_done = True
                    if gather:
                        waits = []
                        for (_, SG_) in stg:
                            if SG_.dsem is not None:
                                P._wait_dma("gpsimd", waits, SG_)
                        P.ops["gpsimd"].append((waits, None, None))
                        for i in range(NGCH):
                            P.ops["gpsimd"].append(([], o_gather(T["gth"], T["gat"], i), (K.ccsem[l], 1)))
                wload(gi + 2)
                wt, WT = wts[gi % 3]
                if kind in ("rope", "g"):
                    for cb in range(4):
                        c = g * 4 + cb
                        for (t0, n) in tts:
                            if t0 >= TL and qpart and last:
                                continue
                            pt, PT = pacc[cnt["p"] % 4]
                            cnt["p"] += 1
                            for kc in range(16):
                                P.op("tensor", o_mm(pt[:, 0:n], wt[:, kc, cb * 128:(cb + 1) * 128],
                                                    hxT[:, kc, t0:t0 + n], kc == 0, kc == 15),
                                     reads=(WT, HXA[t0 // 512], HXV[t0 // 512]), writes=(PT,), inc=(kc == 15))
                            sg, SG = stg[cnt["s"] % 4]
                            cnt["s"] += 1
                            if kind == "g":
                                P.op("scalar", o_act(sg[:, 0:n], pt[:, 0:n], AF.Sigmoid), reads=(PT,), writes=(SG,))
                            elif t0 >= TL:
                                P.op("scalar", o_acopy(sg[:, 0:n], pt[:, 0:n]), reads=(PT,), writes=(SG,))
                            else:
                                t1, T1 = tm1[cnt["t"] % 2]
                                t2, T2 = tm2[cnt["t"] % 2]
                                cnt["t"] += 1
                                P.op("vector", o_tt(t1[:], pt[:], ropeC[:, t0:t0 + n], ALU.mult),
                                     reads=(PT, ROPEC), writes=(T1,))
                                P.op("vector", o_tt(t2[0:64, :], pt[64:128, :], ropeS[0:64, t0:t0 + n], ALU.mult),
                                     reads=(PT, ROPES), writes=(T2,))
                                P.op("vector", o_tt(t2[64:128, :], pt[0:64, :], ropeS[64:128, t0:t0 + n], ALU.mult),
                                     reads=(PT, ROPES), writes=(T2,))
                                P.op("gpsimd", o_tt(sg[:], t1[:], t2[:], ALU.add), reads=(T1, T2), writes=(SG,))
                            P.dma("sync", o_dma(fm_dest(name, c, t0, n), sg[:, 0:n]), SG, False)
                else:
                    for t in range(NTILE):
                        if t >= 16 and qpart and last:
                            continue
                        pt, PT = pacc[cnt["p"] % 4]
                        cnt["p"] += 1
                        for kc in range(16):
                            P.op("tensor", o_mm(pt[:], hxT[:, kc, t * 128:(t + 1) * 128], wt[:, kc, :],
                                                kc == 0, kc == 15), reads=(WT, HXA[t // 4], HXV[t // 4]), writes=(PT,), inc=(kc == 15))
                        sg, SG = stg[cnt["s"] % 4]
                        cnt["s"] += 1
                        if kind == "z":
                            P.op("scalar", o_act(sg[:], pt[:], AF.Silu), reads=(PT,), writes=(SG,))
                        else:
                            P.op("vector", o_copy(sg[:], pt[:]), reads=(PT,), writes=(SG,))
                        P.dma("sync", o_dma(tm_dest(name, g, t), sg[:]), SG, False)
        if gather:
            P.ops["gpsimd"].append(([(K.ccsem[l], NGCH)], None, None))
    run_phase(K, body)


def phase_attnA(K, l, last):
    T = K.T[l]

    def body(P, sb, ps):
        idb, IDB = make_ident(P, sb, BF16)
        NKC = TL + 8 * 128 + CTX
        kta, KTA = sb("kta", [128, 4, NKC], BF16)
        NVB = 16 + 8 + 2
        va, VA = sb("va", [128, NVB, 4, 129], BF16)
        P.op("vector", o_memset(va[:, :, :, 128:129], 1.0), writes=(VA,))
        gth, gat = T["gth"], T["gat"]
        for h in range(4):
            P.dma("sync", o_dma(kta[:, h, 0:TL], gth[R_KTA + h * 128:R_KTA + (h + 1) * 128, :]), KTA, True)
            for r in range(4):
                src = gat_rows(gat, r, R_KTA + h * 128, 128)
                P.dma("sync", o_dma(kta[:, h, TL + r * 128:TL + (r + 1) * 128], src[:, TL - 128:TL]), KTA, True)
                P.dma("sync", o_dma(kta[:, h, TL + (4 + r) * 128:TL + (5 + r) * 128], src[:, 0:128]), KTA, True)
            P.dma("sync", o_dma(kta[:, h, TL + 1024:TL + 1024 + CTX], T["kvc_k_a"][h, :, :]), KTA, True)
        for h in range(4):
            P.dma("sync", o_dma(va[:, 0:16, h, 0:128],
                                bass.AP(gth.tensor, R_VA * 2048 + h * 128, [[512, 128], [128 * 512, 16], [1, 128]])),
                  VA, True)
        def va_blk(r, j):
            row = R_VA + j * 32
            off = ((row // GCH * 4 + r) * GCH + row % GCH) * 2048
            return bass.AP(gat.tensor, off, [[512, 128], [128, 4], [1, 128]])
        for r in range(4):
            P.dma("sync", o_dma(va[:, 16 + r, :, 0:128], va_blk(r, 15)), VA, True)
            P.dma("sync", o_dma(va[:, 20 + r, :, 0:128], va_blk(r, 0)), VA, True)
        for cbk in range(2):
            P.dma("sync", o_dma(va[:, 24 + cbk, :, 0:128],
                                T["kvc_v_a"][cbk * 128:(cbk + 1) * 128, :].rearrange("p (h e) -> p h e", h=4)),
                  VA, True)
        mask, MASK = sb("mask", [128, 10, 128], BF16)
        if True:
            mv, MV = sb("mv", [128, 10], F32)
            P.dma("sync", o_dma(mv[:], K.mvalid.partition_broadcast(128)), MV, True)
            trif, TRIF = sb("trif", [128, 2, 128], F32)
            P.op("gpsimd", o_memset(trif[:], 1.0), writes=(TRIF,))
            P.op("gpsimd", lambda e: e.affine_select(out=trif[:, 0, :], in_=trif[:, 0, :], pattern=[[-1, 128]],
                                                      compare_op=ALU.is_ge, fill=0.0, base=0, channel_multiplier=1),
                 reads=(TRIF,), writes=(TRIF,))
            P.op("gpsimd", lambda e: e.affine_select(out=trif[:, 1, :], in_=trif[:, 1, :], pattern=[[1, 128]],
                                                      compare_op=ALU.is_ge, fill=0.0, base=0, channel_multiplier=-1),
                 reads=(TRIF,), writes=(TRIF,))
            for m in range(10):
                tri = 0 if (m < 4 or m == 8) else 1
                P.op("vector", o_ts(mask[:, m, :], trif[:, tri, :], mv[:, m:m + 1], ALU.mult),
                     reads=(TRIF, MV), writes=(MASK,))
        mbias, MBIAS = sb("mbias", [128, 10, 512], BF16)
        for m in range(10):
            P.op("vector", o_ts(mbias[:, m, :].rearrange("p (g q) -> p g q", g=4),
                                mask[:, m:m + 1, :].broadcast_to([128, 4, 128]), -1.0, ALU.add, 30000.0, ALU.mult),
                 reads=(MASK,), writes=(MBIAS,))
        esk, ESK = sb("esk", [128, 16], F32)
        P.dma("sync", o_dma(esk[:], K.sink[l].partition_broadcast(128)), ESK, True)
        P.op("scalar", o_act(esk[:], esk[:], AF.Exp), reads=(ESK,), writes=(ESK,))

        qts = [sb("qta", [128, 2048], BF16) for _ in range(2)]
        zas = [sb("za", [128, D], BF16) for _ in range(2)]
        gs = [sb("ga", [128, D], BF16) for _ in range(2)]
        gts = [sb("gta", [128, 16, 128], BF16) for _ in range(2)]
        es = [sb("ea", [128, 512], BF16) for _ in range(5)]
        rr, RR = sb("rr", [128, 8], F32)
        pS = [ps("pSa", [128, 512], F32) for _ in range(3)]
        pO = [ps("pOa", [128, 512], F32) for _ in range(4)]
        oas = [sb("oas", [128, 4, 129], F32) for _ in range(2)]
        pT = [ps("pTa", [128, 8, 128], BF16) for _ in range(1)]
        cnt = {"s": 0, "e": 0, "r": 0}
        nqb = 16 if last else 18
        deferredA = []

        def tickA(force=False):
            for d in list(deferredA):
                d[0] -= 1
                if d[0] <= 0 or force:
                    deferredA.remove(d)
                    d[1]()

        def qz_loads(j):
            qt, QT = qts[j % 2]
            za, ZA = zas[j % 2]
            P.dma("sync", o_dma(qt[:].rearrange("p (c t) -> p c t", c=16),
                                T["qt_a"][:, :, j * 128:(j + 1) * 128].rearrange("c p t -> p c t")), QT, True)
            P.dma("sync", o_dma(za[:], T["z_a"][j * 128:(j + 1) * 128, :]), ZA, True)
        qz_loads(0)
        for j in range(nqb):
            qt, QT = qts[j % 2]
            za, ZA = zas[j % 2]
            g_, G_ = gs[j % 2]
            gt, GT = gts[j % 2]
            if j + 1 < nqb:
                qz_loads(j + 1)
            if j >= 16:
                kbl = [(TL + 1024, 24, None), (TL + 1024 + 128, 25, None)]
            else:
                kbl = []
                if j > 0:
                    kbl.append(((j - 1) * 128, j - 1, 8))
                else:
                    kbl += [(TL + r * 128, 16 + r, r) for r in range(4)]
                kbl.append((j * 128, j, None))
                if j < 15:
                    kbl.append(((j + 1) * 128, j + 1, 9))
                else:
                    kbl += [(TL + (4 + r) * 128, 20 + r, 4 + r) for r in range(4)]
                kbl += [(TL + 1024, 24, None), (TL + 1024 + 128, 25, None)]
            seq = [(h, ki, kc0, vb, mi) for h in range(4) for ki, (kc0, vb, mi) in enumerate(kbl)]

            def qk(item):
                h, ki, kc0, vb, mi = item
                s_, S_ = pS[cnt["s"] % 3]
                cnt["s"] += 1
                P.op("tensor", o_mm(s_[:], kta[:, h, kc0:kc0 + 128], qt[:, 4 * h * 128:(4 * h + 4) * 128],
                                    True, mi is None), reads=(KTA, QT), writes=(S_,), inc=(mi is None))
                if mi is not None:
                    P.op("tensor", o_mm(s_[:], idb[:], mbias[:, mi, :], False, True),
                         reads=(IDB, MBIAS), writes=(S_,))
                return s_, S_
            ahead = [qk(seq[0]), qk(seq[1])]
            for idx, (h, ki, kc0, vb, mi) in enumerate(seq):
                tickA()
                s_, S_ = ahead.pop(0)
                e_, E_ = es[cnt["e"] % 5]
                cnt["e"] += 1
                P.op("scalar", o_act(e_[:], s_[:], AF.Exp, scale=SCALE), reads=(S_,), writes=(E_,))
                if idx + 2 < len(seq):
                    ahead.append(qk(seq[idx + 2]))
                for g in range(4):
                    o_, O_ = pO[g]
                    P.op("tensor", o_mm(o_[:, 0:129], e_[:, g * 128:(g + 1) * 128], va[:, vb, h, :],
                                        ki == 0, ki == len(kbl) - 1), reads=(E_, VA), writes=(O_,), inc=(g == 3))
                if ki == len(kbl) - 1:
                    oa4, OA4 = oas[cnt["r"] % 2]
                    rc4 = rr[:, (cnt["r"] % 2) * 4:(cnt["r"] % 2) * 4 + 4]
                    cnt["r"] += 1
                    for g in range(4):
                        P.op("vector", o_copy(oa4[:, g, :], pO[g][0][:, 0:129]), reads=(pO[g][1],), writes=(OA4,))
                    P.op("vector", o_tt(rc4, oa4[:, :, 128], esk[:, 4 * h:4 * h + 4], ALU.add), reads=(OA4, ESK), writes=(RR,))
                    P.op("vector", o_recip(rc4, rc4), reads=(RR,), writes=(RR,))
                    for g in range(4):
                        hd = 4 * h + g
                        P.op("vector", o_stt(g_[:, hd * 128:(hd + 1) * 128], oa4[:, g, 0:128], rc4[:, g:g + 1],
                                             za[:, hd * 128:(hd + 1) * 128], ALU.mult, ALU.mult),
                             reads=(OA4, RR, ZA), writes=(G_,))
            def tail(g_=g_, G_=G_, gt=gt, GT=GT, j=j):
                for half in range(2):
                    t_, T_ = pT[0]
                    for c8 in range(8):
                        c = half * 8 + c8
                        P.op("tensor", o_tr(t_[:, c8, :], g_[:, c * 128:(c + 1) * 128], idb[:]),
                             reads=(G_, IDB), writes=(T_,), inc=(c8 == 7))
                    P.op("vector", o_copy(gt[:, half * 8:(half + 1) * 8, :], t_[:]), reads=(T_,), writes=(GT,))
                P.dma("sync", o_dma(T["gt_a"][:, :, j * 128:(j + 1) * 128].rearrange("c p t -> p c t"), gt[:]), GT, False)
            deferredA.append([10, tail])
        tickA(force=True)
    run_phase(K, body)


def phase_attnB(K, l, last, heads=range(8)):
    T = K.T[l]
    lam_init = 0.8 - 0.6 * math.exp(-0.3 * l)
    NK = CTX + SEQ
    NKB = NK // 128

    def body(P, sb, ps):
        idb, IDB = make_ident(P, sb, BF16)
        lq, LQ = sb("lq", [128, 4, 128], F32)
        P.dma("sync", o_dma(lq[:], K.lam_qk[l].partition_broadcast(128)), LQ, True)
        lsc, LSC = sb("lsc", [128, 8], F32)
        ljk, LJK = sb("ljk", [128, 2, 128], F32)
        for i in range(2):
            P.op("vector", o_tt(ljk[:, i, :], lq[:, 2 * i, :], lq[:, 2 * i + 1, :], ALU.mult), reads=(LQ,), writes=(LJK,))
            P.op("vector", o_rsum(lsc[:, i:i + 1], ljk[:, i, :]), reads=(LJK,), writes=(LSC,))
        P.op("scalar", o_act(lsc[:, 0:2], lsc[:, 0:2], AF.Exp), reads=(LSC,), writes=(LSC,))
        P.op("vector", o_tt(lsc[:, 2:3], lsc[:, 0:1], lsc[:, 1:2], ALU.subtract), reads=(LSC,), writes=(LSC,))
        P.op("vector", o_ts(lsc[:, 3:4], lsc[:, 2:3], lam_init, ALU.add), reads=(LSC,), writes=(LSC,))
        lam = lsc[:, 3:4]
        gsub, GSUB = sb("gsub", [128, 256], F32)
        P.dma("sync", o_dma(gsub[:], K.g_subln[l].partition_broadcast(128)), GSUB, True)
        P.op("vector", o_ts(gsub[:], gsub[:], 1.0 - lam_init, ALU.mult), reads=(GSUB,), writes=(GSUB,))

        kts = [sb("ktb", [128, 2, NK], BF16) for _ in range(2)]
        vbs = [sb("vb", [128, NKB, 257], BF16) for _ in range(2)]
        for v_, V_ in vbs:
            P.op("vector", o_memset(v_[:, :, 256:257], 1.0), writes=(V_,))
        qbs = [sb("qtb", [128, 2, NT], BF16) for _ in range(2)]
        zbs = [sb("zb", [128, 2, 256], BF16) for _ in range(2)]
        es = [sb("eb", [128, 512], BF16) for _ in range(6)]
        sm, SM = sb("smb", [128, 16], F32)
        tmp, TMP = sb("tmpb", [128, 256], F32)
        ot, OT = sb("ob", [128, 256], F32)
        sq, SQ = sb("sqb", [128, 256], F32)
        gz, GZ = sb("gzb", [128, 256], F32)
        gbs = [sb("gb", [128, 256], BF16) for _ in range(4)]
        gtb = [sb("gtb", [128, 2, 256], BF16) for _ in range(2)]
        pS = [ps("pSb", [128, 512], F32) for _ in range(3)]
        pO = [ps("pOb", [128, 512], F32) for _ in range(4)]
        pT, PT = ps("pTb", [128, 2, 256], BF16)
        gat = T["gat"]
        cnt = {"s": 0, "e": 0, "z": 0, "g": 0, "t": 0}
        nqt = 8 if last else 9
        osb = [sb("osb", [128, 257], F32) for _ in range(4)]
        mhalf, MHALF = sb("mhalf", [128, 1], F32)
        P.op("gpsimd", o_memset(mhalf[:], -0.5), writes=(MHALF,))
        heads_l = list(heads)

        def kv_loads(hi):
            h = heads_l[hi]
            kt, KT = kts[hi % 2]
            vb, VB = vbs[hi % 2]
            qb, QB = qbs[hi % 2]
            for m in range(2):
                c = 2 * h + m
                P.dma("sync", o_dma(kt[:, m, 0:CTX], T["kvc_k_b"][c, :, :]), KT, True)
                for r in range(4):
                    P.dma("sync", o_dma(kt[:, m, CTX + r * TL:CTX + (r + 1) * TL],
                                        gat_rows(gat, r, R_KTB + c * 128, 128)), KT, True)
                P.dma("sync", o_dma(qb[:, m, :], T["qt_b"][c, :, :]), QB, True)
            P.dma("sync", o_dma(vb[:, 0:2, 0:256],
                                T["kvc_v_b"][:, h * 256:(h + 1) * 256].rearrange("(b p) e -> p b e", p=128)), VB, True)
            for r in range(4):
                for i in range(TL // GCH):
                    P.dma("sync", o_dma(vb[:, 2 + r * 16 + 2 * i:4 + r * 16 + 2 * i, 0:256],
                                        gat_rows(gat, r, R_VB + i * GCH, GCH)[:, h * 256:(h + 1) * 256]
                                        .rearrange("(b p) e -> p b e", p=128)), VB, True)

        deferred = []

        def tick(force=False):
            for d in list(deferred):
                d[0] -= 1
                if d[0] <= 0 or force:
                    deferred.remove(d)
                    d[1]()

        def finalize(h, qt, zb, ZB):
            q0 = qt * 256
            gbl = []
            for i in range(4):
                P.op("vector", o_copy(osb[i][0][:], pO[i][0][:, 0:257]), reads=(pO[i][1],), writes=(osb[i][1],))
            for qs in range(2):
                o1, O1 = osb[qs]
                o2, O2 = osb[2 + qs]
                gb, GB = gbs[cnt["g"] % 4]
                cnt["g"] += 1
                P.op("vector", o_recip(sm[:, 0:1], o1[:, 256:257]), reads=(O1,), writes=(SM,))
                P.op("vector", o_recip(sm[:, 1:2], o2[:, 256:257]), reads=(O2,), writes=(SM,))
                P.op("vector", o_tt(sm[:, 1:2], sm[:, 1:2], lam, ALU.mult), reads=(SM, LSC), writes=(SM,))
                P.op("vector", o_ts(tmp[:], o2[:, 0:256], sm[:, 1:2], ALU.mult), reads=(O2, SM), writes=(TMP,))
                P.op("vector", o_stt(ot[:], o1[:, 0:256], sm[:, 0:1], tmp[:], ALU.mult, ALU.subtract),
                     reads=(O1, SM, TMP), writes=(OT,))
                P.op("vector", o_tt(sq[:], ot[:], ot[:], ALU.mult), reads=(OT,), writes=(SQ,))
                P.op("vector", o_rsum(sm[:, 2:3], sq[:]), reads=(SQ,), writes=(SM,))
                P.op("vector", o_ts(sm[:, 3:4], sm[:, 2:3], 1.0 / 256, ALU.mult, EPS, ALU.add),
                     reads=(SM,), writes=(SM,))
                P.op("gpsimd", o_tt(sm[:, 4:5], sm[:, 3:4], mhalf[:], ALU.pow), reads=(SM, MHALF), writes=(SM,))
                P.op("vector", o_tt(gz[:], gsub[:], zb[:, qs, :], ALU.mult), reads=(GSUB, ZB), writes=(GZ,))
                P.op("vector", o_stt(gb[:], ot[:], sm[:, 4:5], gz[:], ALU.mult, ALU.mult),
                     reads=(OT, SM, GZ), writes=(GB,))
                gbl.append((gb, GB))

            def tail(gbl=gbl, h=h, q0=q0):
                for qs, (gb, GB) in enumerate(gbl):
                    for ch in range(2):
                        P.op("tensor", o_tr(pT[:, ch, qs * 128:(qs + 1) * 128], gb[:, ch * 128:(ch + 1) * 128], idb[:]),
                             reads=(GB, IDB), writes=(PT,), inc=(ch == 1))
                g2, G2 = gtb[cnt["t"] % 2]
                cnt["t"] += 1
                P.op("vector", o_copy(g2[:], pT[:]), reads=(PT,), writes=(G2,))
                P.dma("sync", o_dma(T["gt_b"][2 * h:2 * h + 2, :, q0:q0 + 256].rearrange("c p t -> p c t"), g2[:]), G2, False)
            deferred.append([24, tail])

        kv_loads(0)
        for hi, h in enumerate(heads_l):
            kt, KT = kts[hi % 2]
            vb, VB = vbs[hi % 2]
            qb, QB = qbs[hi % 2]
            seq = [(qt, ki, kb, nk) for qt in range(nqt)
                   for nk, kl in [(NKB, range(NKB)) if qt < 8 else (2, range(2))] for ki, kb in enumerate(kl)]

            def qk(item):
                qt, ki, kb, nk = item
                s_, S_ = pS[cnt["s"] % 3]
                cnt["s"] += 1
                for m in range(2):
                    P.op("tensor", o_mm(s_[:, m * 256:(m + 1) * 256], kt[:, m, kb * 128:(kb + 1) * 128],
                                        qb[:, m, qt * 256:(qt + 1) * 256], True, True), reads=(KT, QB), writes=(S_,),
                         inc=(m == 1))
                return s_, S_
            ahead = [qk(seq[0]), qk(seq[1])]
            zcur = None
            for idx, (qt, ki, kb, nk) in enumerate(seq):
                if ki == 0:
                    zb, ZB = zbs[cnt["z"] % 2]
                    cnt["z"] += 1
                    q0 = qt * 256
                    P.dma("sync", o_dma(zb[:], T["z_b"][q0:q0 + 256, h * 256:(h + 1) * 256]
                                        .rearrange("(s p) e -> p s e", p=128)), ZB, True)
                    zcur = (zb, ZB)
                s_, S_ = ahead.pop(0)
                e_, E_ = es[cnt["e"] % 6]
                cnt["e"] += 1
                P.op("scalar", o_act(e_[:], s_[:], AF.Exp, scale=SCALE), reads=(S_,), writes=(E_,))
                if idx + 2 < len(seq):
                    ahead.append(qk(seq[idx + 2]))
                for m in range(2):
                    for qs in range(2):
                        o_, O_ = pO[m * 2 + qs]
                        P.op("tensor", o_mm(o_[:, 0:257], e_[:, m * 256 + qs * 128:m * 256 + (qs + 1) * 128],
                                            vb[:, kb, :], ki == 0, ki == nk - 1), reads=(E_, VB), writes=(O_,),
                             inc=(m == 1 and qs == 1))
                tick()
                if ki == nk - 1:
                    finalize(h, qt, zcur[0], zcur[1])
                    if qt == 0 and hi + 1 < len(heads_l):
                        kv_loads(hi + 1)
        tick(force=True)
    run_phase(K, body)


def phase_C(K, l, last, xsrc, xdst):
    T = K.T[l]
    passes = [(0, 1024), (1024, 1024)] + ([] if last else [(2048, 256)])
    for (p0, pn) in passes:
        hold = {}

        def step1(P, sb, ps, p0=p0, pn=pn):
            mT, MT = hold["mT"]
            gta, _ = sb("gta", [128, 16, pn], BF16)
            gtb, _ = sb("gtb", [128, 16, pn], BF16)
            GTAS = [Buf(f"gta{q}") for q in range(4)]
            GTBS = [Buf(f"gtb{q}") for q in range(4)]
            for q in range(4):
                P.dma("sync", o_dma(gta[:, 4 * q:4 * q + 4, :],
                                    T["gt_a"][4 * q:4 * q + 4, :, p0:p0 + pn].rearrange("c p t -> p c t")), GTAS[q], True)
            for q in range(4):
                P.dma("sync", o_dma(gtb[:, 4 * q:4 * q + 4, :],
                                    T["gt_b"][4 * q:4 * q + 4, :, p0:p0 + pn].rearrange("c p t -> p c t")), GTBS[q], True)
            wva = K.w_proj_a[l].rearrange("(k p) c -> p k c", p=128)
            wvb = K.w_proj_b[l].rearrange("(k p) c -> p k c", p=128)
            was = [sb("wpa", [128, 16, 512], BF16) for _ in range(2)]
            wbs = [sb("wpb", [128, 16, 512], BF16) for _ in range(2)]
            sgs = [sb("sg", [128, 2, 512], BF16) for _ in range(2)]
            t1s = [sb("mt1", [128, 512], F32) for _ in range(2)]
            t2s = [sb("mt2", [128, 512], F32) for _ in range(2)]
            pA = [ps("pA", [128, 512], F32) for _ in range(2)]
            pB = [ps("pB", [128, 512], F32) for _ in range(2)]
            n = 0
            tn = min(512, pn)
            def wab_load(g):
                if g < 4:
                    P.dma("gpsimd", o_dma(was[g % 2][0][:], wva[:, :, g * 512:(g + 1) * 512]), was[g % 2][1], True)
                    P.dma("gpsimd", o_dma(wbs[g % 2][0][:], wvb[:, :, g * 512:(g + 1) * 512]), wbs[g % 2][1], True)
            wab_load(0)
            for g in range(4):
                wa, WA = was[g % 2]
                wb, WB = wbs[g % 2]
                wab_load(g + 1)
                for cb in range(4):
                    c = g * 4 + cb
                    for tt in range(pn // tn):
                        ts = slice(tt * tn, (tt + 1) * tn)
                        ya, YA = pA[n % 2]
                        yb, YB = pB[n % 2]
                        sg, SG = sgs[n % 2]
                        t1, T1 = t1s[n % 2]
                        t2, T2 = t2s[n % 2]
                        n += 1
                        P.dma("sync", o_dma(sg[:, 0, 0:tn], T["sg_a"][c, :, p0 + tt * tn:p0 + (tt + 1) * tn]), SG, True)
                        P.dma("sync", o_dma(sg[:, 1, 0:tn], T["sg_b"][c, :, p0 + tt * tn:p0 + (tt + 1) * tn]), SG, True)
                        for kc in range(16):
                            P.op("tensor", o_mm(ya[:, 0:tn], wa[:, kc, cb * 128:(cb + 1) * 128], gta[:, kc, ts],
                                                kc == 0, kc == 15), reads=(WA, GTAS[kc // 4]), writes=(YA,), inc=(kc == 15))
                        for kc in range(16):
                            P.op("tensor", o_mm(yb[:, 0:tn], wb[:, kc, cb * 128:(cb + 1) * 128], gtb[:, kc, ts],
                                                kc == 0, kc == 15), reads=(WB, GTBS[kc // 4]), writes=(YB,), inc=(kc == 15))
                        P.op("vector", o_tt(t1[:, 0:tn], ya[:, 0:tn], sg[:, 0, 0:tn], ALU.mult), reads=(YA, SG), writes=(T1,))
                        P.op("vector", o_tt(t2[:, 0:tn], yb[:, 0:tn], sg[:, 1, 0:tn], ALU.mult), reads=(YB, SG), writes=(T2,))
                        P.op("gpsimd", o_tt(mT[:, c, ts], t1[:, 0:tn], t2[:, 0:tn], ALU.add), reads=(T1, T2), writes=(MT,))

        def step2(P, sb, ps, p0=p0, pn=pn):
            mT, MT = hold["mT"]
            r = 0 if p0 < TL else 1
            wvo = K.w_out[l].rearrange("(k p) c -> p k c", p=128)
            wo, _ = sb("wo", [128, 4, 16, 512], BF16)
            WOS = [Buf(f"wo{g}") for g in range(4)]
            for g in range(4):
                P.dma("gpsimd", o_dma(wo[:, g], wvo[:, :, g * 512:(g + 1) * 512]), WOS[g], True)
            gp, GP = sb("gp", [128, D], F32)
            gpo, GPO = sb("gpo", [128, D], F32)
            P.dma("sync", o_dma(gp[:], K.mod[l][r, 2 * D:3 * D].partition_broadcast(128)), GP, True)
            P.dma("sync", o_dma(gpo[:], K.g_post[l].partition_broadcast(128)), GPO, True)
            P.op("vector", o_tt(gp[:], gp[:], gpo[:], ALU.mult), reads=(GP, GPO), writes=(GP,))
            xts = [sb("xr", [128, D], F32) for _ in range(2)]
            ys = [sb("yo", [128, D], F32) for _ in range(2)]
            junk, JUNK = sb("junk", [128, 512], BF16)
            st_, ST = sb("stc", [128, 8], F32)
            pY = [ps("pY", [128, 512], F32) for _ in range(8)]
            for t in range(pn // 128):
                xt, XT = xts[t % 2]
                y, Y = ys[t % 2]
                tok = p0 + t * 128
                P.dma("sync", o_dma(xt[:], xsrc[tok:tok + 128, :]), XT, True)
                banks = [pY[(t % 2) * 4 + g] for g in range(4)]
                for g in range(4):
                    b_, B_ = banks[g]
                    for kc in range(16):
                        P.op("tensor", o_mm(b_[:], mT[:, kc, t * 128:(t + 1) * 128], wo[:, g, kc, :], kc == 0, kc == 15),
                             reads=(MT, WOS[g]), writes=(B_,), inc=(kc == 15))
                    P.op("scalar", o_act(junk[:], b_[:], AF.Square, accum_out=st_[:, g:g + 1]),
                         reads=(B_,), writes=(JUNK, ST))
                P.op("vector", o_rsum(st_[:, 4:5], st_[:, 0:4]), reads=(ST,), writes=(ST,))
                P.op("vector", o_ts(st_[:, 5:6], st_[:, 4:5], 1.0 / D, ALU.mult, EPS, ALU.add), reads=(ST,), writes=(ST,))
                P.op("scalar", o_act(st_[:, 5:6], st_[:, 5:6], AF.Ln), reads=(ST,), writes=(ST,))
                P.op("scalar", o_act(st_[:, 6:7], st_[:, 5:6], AF.Exp, scale=-0.5), reads=(ST,), writes=(ST,))
                for g in range(4):
                    b_, B_ = banks[g]
                    cs = slice(g * 512, (g + 1) * 512)
                    P.op("vector", o_stt(y[:, cs], b_[:], st_[:, 6:7], gp[:, cs], ALU.mult, ALU.mult),
                         reads=(B_, ST, GP), writes=(Y,))
                P.op("gpsimd", o_tt(y[:], y[:], xt[:], ALU.add), reads=(Y, XT), writes=(Y,))
                P.dma("sync", o_dma(xdst[tok:tok + 128, :], y[:]), Y, False)

        with ExitStack() as st0:
            K.pid += 1
            mt = st0.enter_context(K.nc.sbuf_tensor(f"mT_{K.pid}", [128, 16, pn], BF16))
            hold["mT"] = (mt, Buf("mT"))
            run_phase(K, step1)
            run_phase(K, step2)


SCR = {
    "gth": ([GROWS, 2048], BF16), "gat": ([NGCH, 4, GCH, 2048], BF16),
    "kvc_k_a": ([4, 128, CTX], BF16), "kvc_k_b": ([16, 128, CTX], BF16),
    "kvc_v_a": ([CTX, 512], BF16), "kvc_v_b": ([CTX, 2048], BF16),
    "qt_a": ([16, 128, NT], BF16), "qt_b": ([16, 128, NT], BF16),
    "z_a": ([NT, D], BF16), "z_b": ([NT, D], BF16),
    "sg_a": ([16, 128, NT], BF16), "sg_b": ([16, 128, NT], BF16),
    "gt_a": ([16, 128, NT], BF16), "gt_b": ([16, 128, NT], BF16),
}
WSHAPES = {"w_ada": [D, 3 * D], "b_ada": [3 * D], "g_pre": [D], "g_post": [D], "w_in": [D, IN_COLS],
           "sink": [16], "lam_qk": [4, 128], "g_subln": [256], "w_proj_a": [D, D], "w_proj_b": [D, D],
           "w_out": [D, D]}


def new_ctx():
    K = Ctx()
    K.nc = bass.Bass("TRN2", target_bir_lowering=False)
    K.pid = 0
    K.marks = []
    K.ecount = {e: 0 for e in CENGS}
    K.waited = {e: {} for e in ENGS}
    K.T = {0: {}, 1: {}}
    return K


def declare(K, name, shape, dt, kind):
    return K.nc.dram_tensor(name, list(shape), dt, kind=kind).ap()


def declare_weights(K, layers, names):
    for n in names:
        if not hasattr(K, n):
            setattr(K, n, {})
        for l in layers:
            getattr(K, n)[l] = declare(K, f"{n}{l}", WSHAPES[n], F32, "ExternalInput")


def build_program():
    K = new_ctx()
    K.mod = {}
    K.xall = declare(K, "xall", [NT, D], F32, "ExternalInput")
    K.cc = declare(K, "cc", [2, D], F32, "ExternalInput")
    K.rowbase = declare(K, "rowbase", [64, 1], F32, "ExternalInput")
    K.mvalid = declare(K, "mvalid", [10], F32, "ExternalInput")
    K.ropeC = declare(K, "ropeC", [128, TL], F32, "Internal")
    K.ropeS = declare(K, "ropeS", [128, TL], F32, "Internal")
    declare_weights(K, range(DEPTH), list(WSHAPES))
    for ll in range(DEPTH):
        K.mod[ll] = declare(K, f"mod{ll}", [2, 3 * D], F32, "Internal")
        for n in SCR:
            K.T[ll][n] = declare(K, f"{n}{ll}", SCR[n][0], SCR[n][1], "Internal")
    K.x1 = declare(K, "x1", [NT, D], F32, "Internal")
    K.xout = declare(K, "xout", [TL, D], F32, "ExternalOutput")
    with ExitStack() as st:
        K.esets = {ll: {e: st.enter_context(K.nc.semaphore(f"es{ll}_{e}")) for e in CENGS} for ll in range(DEPTH)}
        K.ccsem = {ll: st.enter_context(K.nc.semaphore(f"cc{ll}")) for ll in range(DEPTH)}
        K.dslots = [[st.enter_context(K.nc.semaphore(f"ds{i}")), 0] for i in range(40)]
        use_layer_sems(K, 0)
        phase_rope(K)
        for ll in range(DEPTH):
            phase_adaln(K, ll)
        for ll in range(DEPTH):
            lst = ll == DEPTH - 1
            if ll > 0:
                use_layer_sems(K, ll)
            xsrc = K.xall if ll == 0 else K.x1
            xdst = K.xout if lst else K.x1
            phase_A(K, ll, lst, xsrc, gather=True)
            phase_attnA(K, ll, lst)
            phase_attnB(K, ll, lst)
            phase_C(K, ll, lst, xsrc, xdst)
    return K.nc


_PROG = {}


def kernel(x, c, ctx, c_ctx, w_ada, b_ada, g_pre, g_post, w_in, sink, lam_qk, g_subln,
           w_proj_a, w_proj_b, w_out):
    f = lambda a: np.ascontiguousarray(np.asarray(a, dtype=np.float32))
    x, c, ctx, c_ctx = f(x), f(c), f(ctx), f(c_ctx)
    W = {"w_ada": f(w_ada), "b_ada": f(b_ada), "g_pre": f(g_pre), "g_post": f(g_post), "w_in": f(w_in),
         "sink": f(sink), "lam_qk": f(lam_qk), "g_subln": f(g_subln), "w_proj_a": f(w_proj_a),
         "w_proj_b": f(w_proj_b), "w_out": f(w_out)}
    cores = list(range(8))
    if "nc" not in _PROG:
        _PROG["nc"] = build_program()
    ins = []
    for r in cores:
        b, s = r // 4, r % 4
        rowbase = np.zeros((64, 1), np.float32)
        rowbase[:32] = s * (TL // 64)
        mvalid = np.zeros((10,), np.float32)
        for q in range(4):
            mvalid[q] = 1.0 if q == s - 1 else 0.0
            mvalid[4 + q] = 1.0 if q == s + 1 else 0.0
        mvalid[8:] = 1.0
        d = {"xall": np.concatenate([x[b, s * TL:(s + 1) * TL], ctx[b]], 0),
             "cc": np.stack([c[b], c_ctx], 0), "rowbase": rowbase, "mvalid": mvalid}
        for n in WSHAPES:
            for l in range(DEPTH):
                d[f"{n}{l}"] = W[n][l]
        ins.append(d)
    res = run_bass_kernel_spmd(_PROG["nc"], ins, core_ids=cores).results
    out = np.zeros((2, SEQ, D), np.float32)
    for r in cores:
        out[r // 4, (r % 4) * TL:(r % 4 + 1) * TL] = res[r]["xout"]
    return out
```

```python
import math
from contextlib import ExitStack

import numpy as np
import concourse.bass as bass
import concourse.mybir as mybir
from concourse.bass_utils import run_bass_kernel_spmd

F32 = mybir.dt.float32
BF16 = mybir.dt.bfloat16
AF = mybir.ActivationFunctionType
ALU = mybir.AluOpType
AX = mybir.AxisListType

D = 2048
SEQ = 8192
CTX = 256
TL = 2048
NT = TL + CTX
NTILE = NT // 128
DEPTH = 2
IN_COLS = 17408
EPS = 1e-6
SCALE = 128 ** -0.5
GROWS = 5120
R_KTA, R_KTB, R_VA, R_VB = 0, 512, 2560, 3072

CENGS = ("tensor", "vector", "scalar", "gpsimd")
ENGS = CENGS + ("sync",)
GCH = 256
NGCH = GROWS // GCH


def gat_rows(gat, r, row0, n):
    assert row0 // GCH == (row0 + n - 1) // GCH
    return gat[row0 // GCH, r, row0 % GCH:row0 % GCH + n, :]


class Buf:
    __slots__ = ("name", "lw", "rd", "dsem", "dtotal", "dsynced", "dbase", "slot", "_stored")

    def __init__(self, name):
        self.name = name
        self.lw = None
        self.rd = {}
        self.dsem = None
        self.dtotal = 0
        self.dsynced = {}
        self.dbase = 0
        self.slot = None


class Plan:
    nsem = 0

    def __init__(self, K, new_sem):
        self.nc = K.nc
        self.new_sem = new_sem
        self.esem = K.esem
        self.ecount = K.ecount
        self.waited = K.waited
        self.ops = {e: [] for e in ENGS}
        self.dbufs = []
        self.noinc_pending = {e: False for e in CENGS}

    def _wait_eng(self, E, waits, F, seq):
        w = self.waited[E]
        if w.get(F, 0) >= seq:
            return
        w[F] = seq
        waits.append((self.esem[F], seq))

    def _wait_dma(self, E, waits, b):
        if b.dtotal > b.dsynced.get(E, b.dbase):
            b.dsynced[E] = b.dtotal
            waits.append((b.dsem, b.dtotal))

    def _deps(self, E, reads, writes, dma_load=False):
        waits = []
        for b in reads:
            if b.lw is not None:
                self._wait_eng(E, waits, b.lw[0], b.lw[1])
            self._wait_dma(E, waits, b)
        for b in writes:
            if b.lw is not None and b.lw[0] != E:
                self._wait_eng(E, waits, b.lw[0], b.lw[1])
            for F, s in b.rd.items():
                if F != E:
                    self._wait_eng(E, waits, F, s)
            if not dma_load:
                self._wait_dma(E, waits, b)
        return waits

    def op(self, E, fn, reads=(), writes=(), inc=True):
        waits = self._deps(E, reads, writes)
        if inc:
            self.ecount[E] += 1
            seq = self.ecount[E]
        else:
            seq = self.ecount[E] + 1
        self.noinc_pending[E] = not inc
        for b in reads:
            b.rd[E] = seq
        for b in writes:
            b.lw = (E, seq)
            b.rd = {}
        self.ops[E].append((waits, fn, (self.esem[E], 1) if inc else None))

    def dma(self, Q, fn, buf, load):
        if buf.dsem is None:
            buf.slot = self.new_sem(buf.name)
            buf.dsem = buf.slot[0]
            buf.dbase = buf.dtotal = buf.slot[1]
            self.dbufs.append(buf)
        waits = self._deps(Q, () if load else (buf,), (buf,) if load else (), dma_load=load)
        if load:
            assert not getattr(buf, "_stored", False)
            buf.lw = None
            buf.rd = {}
        else:
            buf._stored = True
        buf.dtotal += 16
        self.ops[Q].append((waits, fn, (buf.dsem, 16)))

    def finish(self, Q="sync"):
        assert not any(self.noinc_pending.values()), self.noinc_pending
        waits = []
        for b in self.dbufs:
            self._wait_dma(Q, waits, b)
            b.slot[1] = b.dtotal
        if waits:
            self.ops[Q].append((waits, None, None))

    def replay(self, block):
        def mk(E):
            ops = self.ops[E]

            def body(eng):
                for waits, fn, inc in ops:
                    for sem, val in waits:
                        eng.wait_ge(sem, val)
                    if fn is not None:
                        ins = fn(eng)
                        if inc is not None:
                            ins.then_inc(inc[0], inc[1])
            return body
        for E in ENGS:
            if self.ops[E]:
                getattr(block, E)(mk(E))


class Ctx:
    pass


def run_phase(K, body):
    nslot = [0]

    def new_slot(name):
        sl = K.dslots[nslot[0]]
        nslot[0] += 1
        return sl
    with ExitStack() as st:
        K.pid += 1
        P = Plan(K, new_slot)
        cnt = [0]

        def sb(name, shape, dt):
            cnt[0] += 1
            t = st.enter_context(K.nc.sbuf_tensor(f"{name}_{K.pid}_{cnt[0]}", list(shape), dt))
            return t, Buf(name)

        def ps(name, shape, dt):
            cnt[0] += 1
            t = st.enter_context(K.nc.psum_tensor(f"{name}_{K.pid}_{cnt[0]}", list(shape), dt))
            return t, Buf(name)
        body(P, sb, ps)
        P.finish()
        K.marks.append((getattr(body, "__qualname__", "?"), dict(K.ecount)))
        with K.nc.Block() as block:
            P.replay(block)


def use_layer_sems(K, l):
    K.esem = K.esets[l]
    K.ecount = {e: 0 for e in CENGS}
    K.waited = {e: {} for e in ENGS}


def o_mm(out, lhsT, rhs, start, stop):
    return lambda e: e.matmul(out, lhsT=lhsT, rhs=rhs, start=start, stop=stop)


def o_tr(out, in_, ident):
    return lambda e: e.transpose(out, in_, ident)


def o_act(out, in_, func, scale=None, bias=None, accum_out=None):
    kw = {}
    if scale is not None:
        kw["scale"] = scale
    if bias is not None:
        kw["bias"] = bias
    if accum_out is not None:
        kw["accum_out"] = accum_out
    return lambda e: e.activation(out=out, in_=in_, func=func, **kw)


def o_tt(out, in0, in1, op):
    return lambda e: e.tensor_tensor(out=out, in0=in0, in1=in1, op=op)


def o_ts(out, in0, s1, op0, s2=None, op1=None):
    if op1 is None:
        return lambda e: e.tensor_scalar(out=out, in0=in0, scalar1=s1, scalar2=None, op0=op0)
    return lambda e: e.tensor_scalar(out=out, in0=in0, scalar1=s1, scalar2=s2, op0=op0, op1=op1)


def o_stt(out, in0, scalar, in1, op0, op1):
    return lambda e: e.scalar_tensor_tensor(out=out, in0=in0, scalar=scalar, in1=in1, op0=op0, op1=op1)


def o_copy(out, in_):
    return lambda e: e.tensor_copy(out=out, in_=in_)


def o_acopy(out, in_):
    return lambda e: e.copy(out=out, in_=in_)


def o_memset(ap, v):
    return lambda e: e.memset(ap, v)


def o_recip(out, in_):
    return lambda e: e.reciprocal(out=out, in_=in_)


def o_rsum(out, in_):
    return lambda e: e.reduce_sum(out=out, in_=in_, axis=AX.X)


def o_dma(out, in_, slow=False):
    if slow:
        return lambda e: e.dma_start(out=out, in_=in_, allow_slow_non_contiguous=True)
    return lambda e: e.dma_start(out=out, in_=in_)


def o_gather(gth, gat, i):
    return lambda e: e.collective_compute("AllGather", ALU.bypass, replica_groups=[[0, 1, 2, 3], [4, 5, 6, 7]],
                                          ins=[gth[i * GCH:(i + 1) * GCH, :]],
                                          outs=[gat[i].rearrange("r g c -> (r g) c")])


def fm_vec(ap1d):
    return ap1d.rearrange("(k p) -> p k", p=128)


def make_ident(P, sb, dt):
    idf, IDF = sb("identf", [128, 128], F32)
    P.op("gpsimd", o_memset(idf[:], 1.0), writes=(IDF,))
    P.op("gpsimd", lambda e: e.affine_select(out=idf[:], in_=idf[:], pattern=[[-1, 128]],
                                              compare_op=ALU.is_equal, fill=0.0, base=0,
                                              channel_multiplier=1), reads=(IDF,), writes=(IDF,))
    if dt == F32:
        return idf, IDF
    idb, IDB = sb("identb", [128, 128], BF16)
    P.op("vector", o_copy(idb[:], idf[:]), reads=(IDF,), writes=(IDB,))
    return idb, IDB


def phase_adaln(K, l):
    def body(P, sb, ps):
        ccT, CCT = sb("ccT", [128, 2, 16], F32)
        for r in range(2):
            P.dma("sync", o_dma(ccT[:, r, :], fm_vec(K.cc[r]), slow=True), CCT, True)
        P.op("scalar", o_act(ccT[:], ccT[:], AF.Silu), reads=(CCT,), writes=(CCT,))
        brow, BROW = sb("brow", [2, 3 * D], F32)
        P.dma("sync", o_dma(brow[:], K.b_ada[l].partition_broadcast(2)), BROW, True)
        mrow, MROW = sb("mrow", [2, 3 * D], F32)
        wv = K.w_ada[l].rearrange("(k p) c -> p k c", p=128)
        wts = [sb("wada", [128, 16, 512], F32) for _ in range(2)]
        pss = [ps("pada", [128, 512], F32) for _ in range(2)]
        for ct in range(12):
            wt, WT = wts[ct % 2]
            pt, PT = pss[ct % 2]
            cs = slice(ct * 512, (ct + 1) * 512)
            P.dma("sync", o_dma(wt[:], wv[:, :, cs]), WT, True)
            for kc in range(16):
                P.op("tensor", o_mm(pt[0:2, :], ccT[:, :, kc], wt[:, kc, :], kc == 0, kc == 15),
                     reads=(CCT, WT), writes=(PT,), inc=(kc == 15))
            P.op("vector", o_tt(mrow[:, cs], pt[0:2, :], brow[:, cs], ALU.add),
                 reads=(PT, BROW), writes=(MROW,))
        P.dma("sync", o_dma(K.mod[l], mrow[:]), MROW, False)
    run_phase(K, body)


def phase_rope(K):
    PI = math.pi

    def body(P, sb, ps):
        pos, POS = sb("pos", [64, TL], F32)
        fidx, FIDX = sb("fidx", [64, 1], F32)
        rb, RB = sb("rowb", [64, 1], F32)
        P.dma("sync", o_dma(rb[:], K.rowbase), RB, True)
        P.op("gpsimd", lambda e: e.iota(pos[0:32, :], pattern=[[1, TL // 64], [0, 64]], base=0, channel_multiplier=0,
                                        allow_small_or_imprecise_dtypes=True), writes=(POS,))
        P.op("gpsimd", lambda e: e.iota(pos[32:64, :], pattern=[[0, TL // 64], [1, 64]], base=0, channel_multiplier=0,
                                        allow_small_or_imprecise_dtypes=True), writes=(POS,))
        P.op("gpsimd", lambda e: e.iota(fidx[0:32, :], pattern=[[0, 1]], base=0, channel_multiplier=1,
                                        allow_small_or_imprecise_dtypes=True), writes=(FIDX,))
        P.op("gpsimd", lambda e: e.iota(fidx[32:64, :], pattern=[[0, 1]], base=0, channel_multiplier=1,
                                        allow_small_or_imprecise_dtypes=True), writes=(FIDX,))
        P.op("scalar", o_act(fidx[:], fidx[:], AF.Exp, scale=-math.log(10000.0) / 32.0), reads=(FIDX,), writes=(FIDX,))
        ang, ANG = sb("ang", [64, TL], F32)
        P.op("vector", o_ts(ang[:], pos[:], rb[:, 0:1], ALU.add, fidx[:, 0:1], ALU.mult),
             reads=(POS, RB, FIDX), writes=(ANG,))
        kf, KF = sb("kf", [64, TL], F32)
        ki, KI = sb("ki", [64, TL], mybir.dt.int32)
        red, RED = sb("red", [64, TL], F32)
        msk, MSK = sb("msk", [64, TL], F32)
        outc, OUTC = sb("outc", [64, TL], F32)
        outs, OUTS = sb("outs", [64, TL], F32)
        outn, OUTN = sb("outn", [64, TL], F32)

        def reduce_sin(shift, out, OUT):
            P.op("vector", o_ts(kf[:], ang[:], shift, ALU.add, 1.0 / (2 * PI), ALU.mult), reads=(ANG,), writes=(KF,))
            P.op("vector", o_copy(ki[:], kf[:]), reads=(KF,), writes=(KI,))
            P.op("vector", o_copy(kf[:], ki[:]), reads=(KI,), writes=(KF,))
            P.op("vector", o_stt(red[:], kf[:], -2 * PI, ang[:], ALU.mult, ALU.add), reads=(KF, ANG), writes=(RED,))
            if shift != 0.0:
                P.op("vector", o_ts(red[:], red[:], shift, ALU.add), reads=(RED,), writes=(RED,))
            P.op("vector", o_ts(msk[:], red[:], PI, ALU.is_gt), reads=(RED,), writes=(MSK,))
            P.op("vector", o_stt(red[:], msk[:], -2 * PI, red[:], ALU.mult, ALU.add), reads=(MSK, RED), writes=(RED,))
            P.op("vector", o_ts(msk[:], red[:], -PI, ALU.is_lt), reads=(RED,), writes=(MSK,))
            P.op("vector", o_stt(red[:], msk[:], 2 * PI, red[:], ALU.mult, ALU.add), reads=(MSK, RED), writes=(RED,))
            P.op("scalar", o_act(out[:], red[:], AF.Sin), reads=(RED,), writes=(OUT,))
        reduce_sin(0.0, outs, OUTS)
        reduce_sin(PI / 2, outc, OUTC)
        P.op("vector", o_ts(outn[:], outs[:], -1.0, ALU.mult), reads=(OUTS,), writes=(OUTN,))
        P.dma("sync", o_dma(K.ropeC[0:64, :], outc[:]), OUTC, False)
        P.dma("sync", o_dma(K.ropeC[64:128, :], outc[:]), OUTC, False)
        P.dma("sync", o_dma(K.ropeS[0:64, :], outn[:]), OUTN, False)
        P.dma("sync", o_dma(K.ropeS[64:128, :], outs[:]), OUTS, False)
    run_phase(K, body)


FAMS = [("k_a", 0, 512, "rope"), ("v_a", 512, 512, "v"), ("k_b", 1024, 2048, "rope"),
        ("v_b", 3072, 2048, "v"), ("q_a", 5120, 2048, "rope"), ("z_a", 7168, 2048, "z"),
        ("q_b", 9216, 2048, "rope"), ("z_b", 11264, 2048, "z"), ("g_a", 13312, 2048, "g"),
        ("g_b", 15360, 2048, "g")]


def phase_A(K, l, last, xsrc, fams=None, gather=False):
    T = K.T[l]

    def body(P, sb, ps):
        hxT, _ = sb("hxT", [128, 16, NT], BF16)
        HXA = [Buf(f"hxTa{i}") for i in range(5)]
        HXV = [Buf(f"hxTv{i}") for i in range(5)]
        idf, IDF = make_ident(P, sb, F32)
        xts = [sb("xt", [128, D], F32) for _ in range(2)]
        for t in range(2):
            P.dma("sync", o_dma(xts[t][0][:], xsrc[t * 128:(t + 1) * 128, :]), xts[t][1], True)
        cst, CST = sb("cst", [128, 5, 16], F32)
        P.dma("sync", o_dma(cst[:, 0, :], fm_vec(K.g_pre[l]), slow=True), CST, True)
        for r in range(2):
            P.dma("sync", o_dma(cst[:, 1 + 2 * r, :], fm_vec(K.mod[l][r, D:2 * D]), slow=True), CST, True)
            P.dma("sync", o_dma(cst[:, 2 + 2 * r, :], fm_vec(K.mod[l][r, 0:D]), slow=True), CST, True)
        for r in range(2):
            P.op("vector", o_stt(cst[:, 1 + 2 * r, :], cst[:, 1 + 2 * r, :], 1.0, cst[:, 0, :], ALU.add, ALU.mult),
                 reads=(CST,), writes=(CST,))
        ropeC, ROPEC = sb("ropeC", [128, TL], F32)
        ropeS, ROPES = sb("ropeS", [128, TL], F32)
        junk, JUNK = sb("junk", [128, D], BF16)
        stat, STAT = sb("stat", [128, 3, NTILE], F32)
        ptr = [ps("ptr", [128, 512], F32) for _ in range(4)]
        nb = 0
        for t in range(NTILE):
            xt, XT = xts[t % 2]
            r = 0 if t < 16 else 1
            if t >= 2:
                P.dma("sync", o_dma(xt[:], xsrc[t * 128:(t + 1) * 128, :]), XT, True)
            if t == 1:
                P.dma("sync", o_dma(ropeC[:], K.ropeC), ROPEC, True)
                P.dma("sync", o_dma(ropeS[:], K.ropeS), ROPES, True)
            P.op("scalar", o_act(junk[:], xt[:], AF.Square, accum_out=stat[:, 0, t:t + 1]),
                 reads=(XT,), writes=(JUNK, STAT))
            P.op("vector", o_ts(stat[:, 1, t:t + 1], stat[:, 0, t:t + 1], 1.0 / D, ALU.mult, EPS, ALU.add),
                 reads=(STAT,), writes=(STAT,))
            P.op("scalar", o_act(stat[:, 1, t:t + 1], stat[:, 1, t:t + 1], AF.Ln), reads=(STAT,), writes=(STAT,))
            P.op("scalar", o_act(stat[:, 2, t:t + 1], stat[:, 1, t:t + 1], AF.Exp, scale=-0.5),
                 reads=(STAT,), writes=(STAT,))
            P.op("vector", o_ts(xt[:], xt[:], stat[:, 2, t:t + 1], ALU.mult), reads=(XT, STAT), writes=(XT,))
            for kq in range(4):
                banks = (ptr[nb % 4], ptr[(nb + 1) % 4])
                nb += 2
                for j in range(4):
                    kc = kq * 4 + j
                    pt, PT = banks[j % 2]
                    P.op("tensor", o_tr(pt[:, (j // 2) * 128:(j // 2 + 1) * 128], xt[:, kc * 128:(kc + 1) * 128], idf[:]),
                         reads=(XT, IDF), writes=(PT,), inc=(j >= 2))
                for j in range(4):
                    kc = kq * 4 + j
                    pt, PT = banks[j % 2]
                    src = pt[:, (j // 2) * 128:(j // 2 + 1) * 128]
                    if j % 2 == 0:
                        P.op("scalar", o_act(hxT[:, kc, t * 128:(t + 1) * 128], src,
                                             AF.Identity, scale=cst[:, 1 + 2 * r, kc:kc + 1],
                                             bias=cst[:, 2 + 2 * r, kc:kc + 1]),
                             reads=(PT, CST), writes=(HXA[t // 4],))
                    else:
                        P.op("vector", o_ts(hxT[:, kc, t * 128:(t + 1) * 128], src,
                                            cst[:, 1 + 2 * r, kc:kc + 1], ALU.mult,
                                            cst[:, 2 + 2 * r, kc:kc + 1], ALU.add),
                             reads=(PT, CST), writes=(HXV[t // 4],))
        wv = K.w_in[l].rearrange("(k p) c -> p k c", p=128)
        wts = [sb("win", [128, 16, 512], BF16) for _ in range(3)]
        pacc = [ps("pacc", [128, 512], F32) for _ in range(4)]
        stg = [sb("stg", [128, 512], BF16) for _ in range(4)]
        tm1 = [sb("tm1", [128, 512], F32) for _ in range(2)]
        tm2 = [sb("tm2", [128, 512], F32) for _ in range(2)]
        cnt = {"w": 0, "p": 0, "s": 0, "t": 0}
        tts = [(0, 512), (512, 512), (1024, 512), (1536, 512), (2048, 256)]

        def fm_dest(name, c, t0, n):
            if name in ("k_a", "k_b"):
                if t0 < TL:
                    r0 = (R_KTA if name == "k_a" else R_KTB) + c * 128
                    return T["gth"][r0:r0 + 128, t0:t0 + n]
                return T["kvc_" + name][c, :, t0 - TL:t0 - TL + n]
            tn = {"q_a": "qt_a", "q_b": "qt_b", "g_a": "sg_a", "g_b": "sg_b"}[name]
            return T[tn][c, :, t0:t0 + n]

        def tm_dest(name, g, t):
            if name in ("v_a", "v_b"):
                if t < 16:
                    if name == "v_a":
                        return bass.AP(T["gth"].tensor, R_VA * 2048 + t * 128 * 512, [[512, 128], [1, 512]])
                    return T["gth"][R_VB + t * 128:R_VB + (t + 1) * 128, g * 512:(g + 1) * 512]
                return T["kvc_" + name][(t - 16) * 128:(t - 15) * 128, g * 512:(g + 1) * 512]
            return T[name][t * 128:(t + 1) * 128, g * 512:(g + 1) * 512]

        groups = [(name, c0, kind, g) for (name, c0, ncols, kind) in FAMS
                  if fams is None or name in fams for g in range(ncols // 512)]

        def wload(i):
            if i < len(groups):
                name_, c0_, _, g_ = groups[i]
                wt_, WT_ = wts[i % 3]
                P.dma("gpsimd", o_dma(wt_[:], wv[:, :, c0_ + g_ * 512:c0_ + (g_ + 1) * 512]), WT_, True)
        wload(0)
        wload(1)
        kv_done = False
        for gi, (name, c0, kind, g) in enumerate(groups):
            if True:
                qpart = c0 >= 5120
                if qpart and not kv_done:
                    kv_done = True
                    if gather:
                        waits = []
                        for (_, SG_) in stg:
                            if SG_.dsem is not None:
                                P._wait_dma("gpsimd", waits, SG_)
                        P.ops["gpsimd"].append((waits, None, None))
                        for i in range(NGCH):
                            P.ops["gpsimd"].append(([], o_gather(T["gth"], T["gat"], i), (K.ccsem[l], 1)))
                wload(gi + 2)
                wt, WT = wts[gi % 3]
                if kind in ("rope", "g"):
                    for cb in range(4):
                        c = g * 4 + cb
                        for (t0, n) in tts:
                            if t0 >= TL and qpart and last:
                                continue
                            pt, PT = pacc[cnt["p"] % 4]
                            cnt["p"] += 1
                            for kc in range(16):
                                P.op("tensor", o_mm(pt[:, 0:n], wt[:, kc, cb * 128:(cb + 1) * 128],
                                                    hxT[:, kc, t0:t0 + n], kc == 0, kc == 15),
                                     reads=(WT, HXA[t0 // 512], HXV[t0 // 512]), writes=(PT,), inc=(kc == 15))
                            sg, SG = stg[cnt["s"] % 4]
                            cnt["s"] += 1
                            if kind == "g":
                                P.op("scalar", o_act(sg[:, 0:n], pt[:, 0:n], AF.Sigmoid), reads=(PT,), writes=(SG,))
                            elif t0 >= TL:
                                P.op("scalar", o_acopy(sg[:, 0:n], pt[:, 0:n]), reads=(PT,), writes=(SG,))
                            else:
                                t1, T1 = tm1[cnt["t"] % 2]
                                t2, T2 = tm2[cnt["t"] % 2]
                                cnt["t"] += 1
                                P.op("vector", o_tt(t1[:], pt[:], ropeC[:, t0:t0 + n], ALU.mult),
                                     reads=(PT, ROPEC), writes=(T1,))
                                P.op("vector", o_tt(t2[0:64, :], pt[64:128, :], ropeS[0:64, t0:t0 + n], ALU.mult),
                                     reads=(PT, ROPES), writes=(T2,))
                                P.op("vector", o_tt(t2[64:128, :], pt[0:64, :], ropeS[64:128, t0:t0 + n], ALU.mult),
                                     reads=(PT, ROPES), writes=(T2,))
                                P.op("gpsimd", o_tt(sg[:], t1[:], t2[:], ALU.add), reads=(T1, T2), writes=(SG,))
                            P.dma("sync", o_dma(fm_dest(name, c, t0, n), sg[:, 0:n]), SG, False)
                else:
                    for t in range(NTILE):
                        if t >= 16 and qpart and last:
                            continue
                        pt, PT = pacc[cnt["p"] % 4]
                        cnt["p"] += 1
                        for kc in range(16):
                            P.op("tensor", o_mm(pt[:], hxT[:, kc, t * 128:(t + 1) * 128], wt[:, kc, :],
                                                kc == 0, kc == 15), reads=(WT, HXA[t // 4], HXV[t // 4]), writes=(PT,), inc=(kc == 15))
                        sg, SG = stg[cnt["s"] % 4]
                        cnt["s"] += 1
                        if kind == "z":
                            P.op("scalar", o_act(sg[:], pt[:], AF.Silu), reads=(PT,), writes=(SG,))
                        else:
                            P.op("vector", o_copy(sg[:], pt[:]), reads=(PT,), writes=(SG,))
                        P.dma("sync", o_dma(tm_dest(name, g, t), sg[:]), SG, False)
        if gather:
            P.ops["gpsimd"].append(([(K.ccsem[l], NGCH)], None, None))
    run_phase(K, body)


def phase_attnA(K, l, last):
    T = K.T[l]

    def body(P, sb, ps):
        idb, IDB = make_ident(P, sb, BF16)
        NKC = TL + 8 * 128 + CTX
        kta, KTA = sb("kta", [128, 4, NKC], BF16)
        NVB = 16 + 8 + 2
        va, VA = sb("va", [128, NVB, 4, 129], BF16)
        P.op("vector", o_memset(va[:, :, :, 128:129], 1.0), writes=(VA,))
        gth, gat = T["gth"], T["gat"]
        for h in range(4):
            P.dma("sync", o_dma(kta[:, h, 0:TL], gth[R_KTA + h * 128:R_KTA + (h + 1) * 128, :]), KTA, True)
            for r in range(4):
                src = gat_rows(gat, r, R_KTA + h * 128, 128)
                P.dma("sync", o_dma(kta[:, h, TL + r * 128:TL + (r + 1) * 128], src[:, TL - 128:TL]), KTA, True)
                P.dma("sync", o_dma(kta[:, h, TL + (4 + r) * 128:TL + (5 + r) * 128], src[:, 0:128]), KTA, True)
            P.dma("sync", o_dma(kta[:, h, TL + 1024:TL + 1024 + CTX], T["kvc_k_a"][h, :, :]), KTA, True)
        for h in range(4):
            P.dma("sync", o_dma(va[:, 0:16, h, 0:128],
                                bass.AP(gth.tensor, R_VA * 2048 + h * 128, [[512, 128], [128 * 512, 16], [1, 128]])),
                  VA, True)
        def va_blk(r, j):
            row = R_VA + j * 32
            off = ((row // GCH * 4 + r) * GCH + row % GCH) * 2048
            return bass.AP(gat.tensor, off, [[512, 128], [128, 4], [1, 128]])
        for r in range(4):
            P.dma("sync", o_dma(va[:, 16 + r, :, 0:128], va_blk(r, 15)), VA, True)
            P.dma("sync", o_dma(va[:, 20 + r, :, 0:128], va_blk(r, 0)), VA, True)
        for cbk in range(2):
            P.dma("sync", o_dma(va[:, 24 + cbk, :, 0:128],
                                T["kvc_v_a"][cbk * 128:(cbk + 1) * 128, :].rearrange("p (h e) -> p h e", h=4)),
                  VA, True)
        mask, MASK = sb("mask", [128, 10, 128], BF16)
        if True:
            mv, MV = sb("mv", [128, 10], F32)
            P.dma("sync", o_dma(mv[:], K.mvalid.partition_broadcast(128)), MV, True)
            trif, TRIF = sb("trif", [128, 2, 128], F32)
            P.op("gpsimd", o_memset(trif[:], 1.0), writes=(TRIF,))
            P.op("gpsimd", lambda e: e.affine_select(out=trif[:, 0, :], in_=trif[:, 0, :], pattern=[[-1, 128]],
                                                      compare_op=ALU.is_ge, fill=0.0, base=0, channel_multiplier=1),
                 reads=(TRIF,), writes=(TRIF,))
            P.op("gpsimd", lambda e: e.affine_select(out=trif[:, 1, :], in_=trif[:, 1, :], pattern=[[1, 128]],
                                                      compare_op=ALU.is_ge, fill=0.0, base=0, channel_multiplier=-1),
                 reads=(TRIF,), writes=(TRIF,))
            for m in range(10):
                tri = 0 if (m < 4 or m == 8) else 1
                P.op("vector", o_ts(mask[:, m, :], trif[:, tri, :], mv[:, m:m + 1], ALU.mult),
                     reads=(TRIF, MV), writes=(MASK,))
        mbias, MBIAS = sb("mbias", [128, 10, 512], BF16)
        for m in range(10):
            P.op("vector", o_ts(mbias[:, m, :].rearrange("p (g q) -> p g q", g=4),
                                mask[:, m:m + 1, :].broadcast_to([128, 4, 128]), -1.0, ALU.add, 30000.0, ALU.mult),
                 reads=(MASK,), writes=(MBIAS,))
        esk, ESK = sb("esk", [128, 16], F32)
        P.dma("sync", o_dma(esk[:], K.sink[l].partition_broadcast(128)), ESK, True)
        P.op("scalar", o_act(esk[:], esk[:], AF.Exp), reads=(ESK,), writes=(ESK,))

        qts = [sb("qta", [128, 2048], BF16) for _ in range(2)]
        zas = [sb("za", [128, D], BF16) for _ in range(2)]
        gs = [sb("ga", [128, D], BF16) for _ in range(2)]
        gts = [sb("gta", [128, 16, 128], BF16) for _ in range(2)]
        es = [sb("ea", [128, 512], BF16) for _ in range(5)]
        rr, RR = sb("rr", [128, 8], F32)
        pS = [ps("pSa", [128, 512], F32) for _ in range(3)]
        pO = [ps("pOa", [128, 512], F32) for _ in range(4)]
        oas = [sb("oas", [128, 4, 129], F32) for _ in range(2)]
        pT = [ps("pTa", [128, 8, 128], BF16) for _ in range(1)]
        cnt = {"s": 0, "e": 0, "r": 0}
        nqb = 16 if last else 18
        deferredA = []

        def tickA(force=False):
            for d in list(deferredA):
                d[0] -= 1
                if d[0] <= 0 or force:
                    deferredA.remove(d)
                    d[1]()

        def qz_loads(j):
            qt, QT = qts[j % 2]
            za, ZA = zas[j % 2]
            P.dma("sync", o_dma(qt[:].rearrange("p (c t) -> p c t", c=16),
                                T["qt_a"][:, :, j * 128:(j + 1) * 128].rearrange("c p t -> p c t")), QT, True)
            P.dma("sync", o_dma(za[:], T["z_a"][j * 128:(j + 1) * 128, :]), ZA, True)
        qz_loads(0)
        for j in range(nqb):
            qt, QT = qts[j % 2]
            za, ZA = zas[j % 2]
            g_, G_ = gs[j % 2]
            gt, GT = gts[j % 2]
            if j + 1 < nqb:
                qz_loads(j + 1)
            if j >= 16:
                kbl = [(TL + 1024, 24, None), (TL + 1024 + 128, 25, None)]
            else:
                kbl = []
                if j > 0:
                    kbl.append(((j - 1) * 128, j - 1, 8))
                else:
                    kbl += [(TL + r * 128, 16 + r, r) for r in range(4)]
                kbl.append((j * 128, j, None))
                if j < 15:
                    kbl.append(((j + 1) * 128, j + 1, 9))
                else:
                    kbl += [(TL + (4 + r) * 128, 20 + r, 4 + r) for r in range(4)]
                kbl += [(TL + 1024, 24, None), (TL + 1024 + 128, 25, None)]
            seq = [(h, ki, kc0, vb, mi) for h in range(4) for ki, (kc0, vb, mi) in enumerate(kbl)]

            def qk(item):
                h, ki, kc0, vb, mi = item
                s_, S_ = pS[cnt["s"] % 3]
                cnt["s"] += 1
                P.op("tensor", o_mm(s_[:], kta[:, h, kc0:kc0 + 128], qt[:, 4 * h * 128:(4 * h + 4) * 128],
                                    True, mi is None), reads=(KTA, QT), writes=(S_,), inc=(mi is None))
                if mi is not None:
                    P.op("tensor", o_mm(s_[:], idb[:], mbias[:, mi, :], False, True),
                         reads=(IDB, MBIAS), writes=(S_,))
                return s_, S_
            ahead = [qk(seq[0]), qk(seq[1])]
            for idx, (h, ki, kc0, vb, mi) in enumerate(seq):
                tickA()
                s_, S_ = ahead.pop(0)
                e_, E_ = es[cnt["e"] % 5]
                cnt["e"] += 1
                P.op("scalar", o_act(e_[:], s_[:], AF.Exp, scale=SCALE), reads=(S_,), writes=(E_,))
                if idx + 2 < len(seq):
                    ahead.append(qk(seq[idx + 2]))
                for g in range(4):
                    o_, O_ = pO[g]
                    P.op("tensor", o_mm(o_[:, 0:129], e_[:, g * 128:(g + 1) * 128], va[:, vb, h, :],
                                        ki == 0, ki == len(kbl) - 1), reads=(E_, VA), writes=(O_,), inc=(g == 3))
                if ki == len(kbl) - 1:
                    oa4, OA4 = oas[cnt["r"] % 2]
                    rc4 = rr[:, (cnt["r"] % 2) * 4:(cnt["r"] % 2) * 4 + 4]
                    cnt["r"] += 1
                    for g in range(4):
                        P.op("vector", o_copy(oa4[:, g, :], pO[g][0][:, 0:129]), reads=(pO[g][1],), writes=(OA4,))
                    P.op("vector", o_tt(rc4, oa4[:, :, 128], esk[:, 4 * h:4 * h + 4], ALU.add), reads=(OA4, ESK), writes=(RR,))
                    P.op("vector", o_recip(rc4, rc4), reads=(RR,), writes=(RR,))
                    for g in range(4):
                        hd = 4 * h + g
                        P.op("vector", o_stt(g_[:, hd * 128:(hd + 1) * 128], oa4[:, g, 0:128], rc4[:, g:g + 1],
                                             za[:, hd * 128:(hd + 1) * 128], ALU.mult, ALU.mult),
                             reads=(OA4, RR, ZA), writes=(G_,))
            def tail(g_=g_, G_=G_, gt=gt, GT=GT, j=j):
                for half in range(2):
                    t_, T_ = pT[0]
                    for c8 in range(8):
                        c = half * 8 + c8
                        P.op("tensor", o_tr(t_[:, c8, :], g_[:, c * 128:(c + 1) * 128], idb[:]),
                             reads=(G_, IDB), writes=(T_,), inc=(c8 == 7))
                    P.op("vector", o_copy(gt[:, half * 8:(half + 1) * 8, :], t_[:]), reads=(T_,), writes=(GT,))
                P.dma("sync", o_dma(T["gt_a"][:, :, j * 128:(j + 1) * 128].rearrange("c p t -> p c t"), gt[:]), GT, False)
            deferredA.append([10, tail])
        tickA(force=True)
    run_phase(K, body)


def phase_attnB(K, l, last, heads=range(8)):
    T = K.T[l]
    lam_init = 0.8 - 0.6 * math.exp(-0.3 * l)
    NK = CTX + SEQ
    NKB = NK // 128

    def body(P, sb, ps):
        idb, IDB = make_ident(P, sb, BF16)
        lq, LQ = sb("lq", [128, 4, 128], F32)
        P.dma("sync", o_dma(lq[:], K.lam_qk[l].partition_broadcast(128)), LQ, True)
        lsc, LSC = sb("lsc", [128, 8], F32)
        ljk, LJK = sb("ljk", [128, 2, 128], F32)
        for i in range(2):
            P.op("vector", o_tt(ljk[:, i, :], lq[:, 2 * i, :], lq[:, 2 * i + 1, :], ALU.mult), reads=(LQ,), writes=(LJK,))
            P.op("vector", o_rsum(lsc[:, i:i + 1], ljk[:, i, :]), reads=(LJK,), writes=(LSC,))
        P.op("scalar", o_act(lsc[:, 0:2], lsc[:, 0:2], AF.Exp), reads=(LSC,), writes=(LSC,))
        P.op("vector", o_tt(lsc[:, 2:3], lsc[:, 0:1], lsc[:, 1:2], ALU.subtract), reads=(LSC,), writes=(LSC,))
        P.op("vector", o_ts(lsc[:, 3:4], lsc[:, 2:3], lam_init, ALU.add), reads=(LSC,), writes=(LSC,))
        lam = lsc[:, 3:4]
        gsub, GSUB = sb("gsub", [128, 256], F32)
        P.dma("sync", o_dma(gsub[:], K.g_subln[l].partition_broadcast(128)), GSUB, True)
        P.op("vector", o_ts(gsub[:], gsub[:], 1.0 - lam_init, ALU.mult), reads=(GSUB,), writes=(GSUB,))

        kts = [sb("ktb", [128, 2, NK], BF16) for _ in range(2)]
        vbs = [sb("vb", [128, NKB, 257], BF16) for _ in range(2)]
        qbs = [sb("qtb", [128, 2, NT], BF16) for _ in range(2)]
        zbs = [sb("zb", [128, 2, 256], BF16) for _ in range(2)]
        es = [sb("eb", [128, 512], BF16) for _ in range(6)]
        sm, SM = sb("smb", [128, 16], F32)
        tmp, TMP = sb("tmpb", [128, 256], F32)
        ot, OT = sb("ob", [128, 256], F32)
        sq, SQ = sb("sqb", [128, 256], F32)
        gz, GZ = sb("gzb", [128, 256], F32)
        gbs = [sb("gb", [128, 256], BF16) for _ in range(4)]
        gtb = [sb("gtb", [128, 2, 256], BF16) for _ in range(2)]
        pS = [ps("pSb", [128, 512], F32) for _ in range(3)]
        pO = [ps("pOb", [128, 512], F32) for _ in range(4)]
        pT, PT = ps("pTb", [128, 2, 256], BF16)
        gat = T["gat"]
        cnt = {"s": 0, "e": 0, "z": 0, "g": 0, "t": 0}
        nqt = 8 if last else 9
        osb = [sb("osb", [128, 257], F32) for _ in range(4)]
        mhalf, MHALF = sb("mhalf", [128, 1], F32)
        P.op("gpsimd", o_memset(mhalf[:], -0.5), writes=(MHALF,))
        heads_l = list(heads)

        KTP = [[Buf(f"ktb{i}_{p}") for p in range(5)] for i in range(2)]
        VBP = [[Buf(f"vbb{i}_{p}") for p in range(5)] for i in range(2)]
        for i in range(2):
            P.op("vector", o_memset(vbs[i][0][:, 0:2, 256:257], 1.0), writes=(VBP[i][0],))
            for r in range(4):
                P.op("vector", o_memset(vbs[i][0][:, 2 + r * 16:2 + (r + 1) * 16, 256:257], 1.0), writes=(VBP[i][1 + r],))

        def kpart(kb):
            return 0 if kb < 2 else 1 + (kb - 2) // 16

        def kv_loads(hi):
            h = heads_l[hi]
            kt = kts[hi % 2][0]
            vb = vbs[hi % 2][0]
            qb, QB = qbs[hi % 2]
            KP, VP = KTP[hi % 2], VBP[hi % 2]
            for m in range(2):
                P.dma("sync", o_dma(qb[:, m, :], T["qt_b"][2 * h + m, :, :]), QB, True)
            for m in range(2):
                P.dma("sync", o_dma(kt[:, m, 0:CTX], T["kvc_k_b"][2 * h + m, :, :]), KP[0], True)
            P.dma("sync", o_dma(vb[:, 0:2, 0:256],
                                T["kvc_v_b"][:, h * 256:(h + 1) * 256].rearrange("(b p) e -> p b e", p=128)), VP[0], True)
            for r in range(4):
                for m in range(2):
                    c = 2 * h + m
                    P.dma("sync", o_dma(kt[:, m, CTX + r * TL:CTX + (r + 1) * TL],
                                        gat_rows(gat, r, R_KTB + c * 128, 128)), KP[1 + r], True)
                for i in range(TL // GCH):
                    P.dma("sync", o_dma(vb[:, 2 + r * 16 + 2 * i:4 + r * 16 + 2 * i, 0:256],
                                        gat_rows(gat, r, R_VB + i * GCH, GCH)[:, h * 256:(h + 1) * 256]
                                        .rearrange("(b p) e -> p b e", p=128)), VP[1 + r], True)

        deferred = []

        def tick(force=False):
            for d in list(deferred):
                d[0] -= 1
                if d[0] <= 0 or force:
                    deferred.remove(d)
                    d[1]()

        def finalize(h, qt, zb, ZB):
            q0 = qt * 256
            gbl = []
            for i in range(4):
                P.op("vector", o_copy(osb[i][0][:], pO[i][0][:, 0:257]), reads=(pO[i][1],), writes=(osb[i][1],))
            for qs in range(2):
                o1, O1 = osb[qs]
                o2, O2 = osb[2 + qs]
                gb, GB = gbs[cnt["g"] % 4]
                cnt["g"] += 1
                P.op("vector", o_recip(sm[:, 0:1], o1[:, 256:257]), reads=(O1,), writes=(SM,))
                P.op("vector", o_recip(sm[:, 1:2], o2[:, 256:257]), reads=(O2,), writes=(SM,))
                P.op("vector", o_tt(sm[:, 1:2], sm[:, 1:2], lam, ALU.mult), reads=(SM, LSC), writes=(SM,))
                P.op("vector", o_ts(tmp[:], o2[:, 0:256], sm[:, 1:2], ALU.mult), reads=(O2, SM), writes=(TMP,))
                P.op("vector", o_stt(ot[:], o1[:, 0:256], sm[:, 0:1], tmp[:], ALU.mult, ALU.subtract),
                     reads=(O1, SM, TMP), writes=(OT,))
                P.op("vector", o_tt(sq[:], ot[:], ot[:], ALU.mult), reads=(OT,), writes=(SQ,))
                P.op("vector", o_rsum(sm[:, 2:3], sq[:]), reads=(SQ,), writes=(SM,))
                P.op("vector", o_ts(sm[:, 3:4], sm[:, 2:3], 1.0 / 256, ALU.mult, EPS, ALU.add),
                     reads=(SM,), writes=(SM,))
                P.op("gpsimd", o_tt(sm[:, 4:5], sm[:, 3:4], mhalf[:], ALU.pow), reads=(SM, MHALF), writes=(SM,))
                P.op("vector", o_tt(gz[:], gsub[:], zb[:, qs, :], ALU.mult), reads=(GSUB, ZB), writes=(GZ,))
                P.op("vector", o_stt(gb[:], ot[:], sm[:, 4:5], gz[:], ALU.mult, ALU.mult),
                     reads=(OT, SM, GZ), writes=(GB,))
                gbl.append((gb, GB))

            def tail(gbl=gbl, h=h, q0=q0):
                for qs, (gb, GB) in enumerate(gbl):
                    for ch in range(2):
                        P.op("tensor", o_tr(pT[:, ch, qs * 128:(qs + 1) * 128], gb[:, ch * 128:(ch + 1) * 128], idb[:]),
                             reads=(GB, IDB), writes=(PT,), inc=(ch == 1))
                g2, G2 = gtb[cnt["t"] % 2]
                cnt["t"] += 1
                P.op("vector", o_copy(g2[:], pT[:]), reads=(PT,), writes=(G2,))
                P.dma("sync", o_dma(T["gt_b"][2 * h:2 * h + 2, :, q0:q0 + 256].rearrange("c p t -> p c t"), g2[:]), G2, False)
            deferred.append([24, tail])

        kv_loads(0)
        for hi, h in enumerate(heads_l):
            kt, KT = kts[hi % 2]
            vb, VB = vbs[hi % 2]
            qb, QB = qbs[hi % 2]
            seq = [(qt, ki, kb, nk) for qt in range(nqt)
                   for nk, kl in [(NKB, range(NKB)) if qt < 8 else (2, range(2))] for ki, kb in enumerate(kl)]

            def qk(item):
                qt, ki, kb, nk = item
                s_, S_ = pS[cnt["s"] % 3]
                cnt["s"] += 1
                for m in range(2):
                    P.op("tensor", o_mm(s_[:, m * 256:(m + 1) * 256], kt[:, m, kb * 128:(kb + 1) * 128],
                                        qb[:, m, qt * 256:(qt + 1) * 256], True, True), reads=(KTP[hi % 2][kpart(kb)], QB), writes=(S_,),
                         inc=(m == 1))
                return s_, S_
            ahead = [qk(seq[0]), qk(seq[1])]
            zcur = None
            for idx, (qt, ki, kb, nk) in enumerate(seq):
                if ki == 0:
                    zb, ZB = zbs[cnt["z"] % 2]
                    cnt["z"] += 1
                    q0 = qt * 256
                    P.dma("sync", o_dma(zb[:], T["z_b"][q0:q0 + 256, h * 256:(h + 1) * 256]
                                        .rearrange("(s p) e -> p s e", p=128)), ZB, True)
                    zcur = (zb, ZB)
                s_, S_ = ahead.pop(0)
                e_, E_ = es[cnt["e"] % 6]
                cnt["e"] += 1
                P.op("scalar", o_act(e_[:], s_[:], AF.Exp, scale=SCALE), reads=(S_,), writes=(E_,))
                if idx + 2 < len(seq):
                    ahead.append(qk(seq[idx + 2]))
                for m in range(2):
                    for qs in range(2):
                        o_, O_ = pO[m * 2 + qs]
                        P.op("tensor", o_mm(o_[:, 0:257], e_[:, m * 256 + qs * 128:m * 256 + (qs + 1) * 128],
                                            vb[:, kb, :], ki == 0, ki == nk - 1), reads=(E_, VBP[hi % 2][kpart(kb)]), writes=(O_,),
                             inc=(m == 1 and qs == 1))
                tick()
                if ki == nk - 1:
                    finalize(h, qt, zcur[0], zcur[1])
                    if qt == 0 and hi + 1 < len(heads_l):
                        kv_loads(hi + 1)
        tick(force=True)
    run_phase(K, body)


def phase_C(K, l, last, xsrc, xdst):
    T = K.T[l]
    passes = [(0, 1024), (1024, 1024)] + ([] if last else [(2048, 256)])
    for (p0, pn) in passes:
        hold = {}

        def step1(P, sb, ps, p0=p0, pn=pn):
            mT, MT = hold["mT"]
            gta, _ = sb("gta", [128, 16, pn], BF16)
            gtb, _ = sb("gtb", [128, 16, pn], BF16)
            GTAS = [Buf(f"gta{q}") for q in range(4)]
            GTBS = [Buf(f"gtb{q}") for q in range(4)]
            for q in range(4):
                P.dma("sync", o_dma(gta[:, 4 * q:4 * q + 4, :],
                                    T["gt_a"][4 * q:4 * q + 4, :, p0:p0 + pn].rearrange("c p t -> p c t")), GTAS[q], True)
            for q in range(4):
                P.dma("sync", o_dma(gtb[:, 4 * q:4 * q + 4, :],
                                    T["gt_b"][4 * q:4 * q + 4, :, p0:p0 + pn].rearrange("c p t -> p c t")), GTBS[q], True)
            wva = K.w_proj_a[l].rearrange("(k p) c -> p k c", p=128)
            wvb = K.w_proj_b[l].rearrange("(k p) c -> p k c", p=128)
            was = [sb("wpa", [128, 16, 512], BF16) for _ in range(2)]
            wbs = [sb("wpb", [128, 16, 512], BF16) for _ in range(2)]
            sgs = [sb("sg", [128, 2, 512], BF16) for _ in range(2)]
            t1s = [sb("mt1", [128, 512], F32) for _ in range(2)]
            t2s = [sb("mt2", [128, 512], F32) for _ in range(2)]
            pA = [ps("pA", [128, 512], F32) for _ in range(2)]
            pB = [ps("pB", [128, 512], F32) for _ in range(2)]
            n = 0
            tn = min(512, pn)
            def wab_load(g):
                if g < 4:
                    P.dma("gpsimd", o_dma(was[g % 2][0][:], wva[:, :, g * 512:(g + 1) * 512]), was[g % 2][1], True)
                    P.dma("gpsimd", o_dma(wbs[g % 2][0][:], wvb[:, :, g * 512:(g + 1) * 512]), wbs[g % 2][1], True)
            wab_load(0)
            for g in range(4):
                wa, WA = was[g % 2]
                wb, WB = wbs[g % 2]
                wab_load(g + 1)
                for cb in range(4):
                    c = g * 4 + cb
                    for tt in range(pn // tn):
                        ts = slice(tt * tn, (tt + 1) * tn)
                        ya, YA = pA[n % 2]
                        yb, YB = pB[n % 2]
                        sg, SG = sgs[n % 2]
                        t1, T1 = t1s[n % 2]
                        t2, T2 = t2s[n % 2]
                        n += 1
                        P.dma("sync", o_dma(sg[:, 0, 0:tn], T["sg_a"][c, :, p0 + tt * tn:p0 + (tt + 1) * tn]), SG, True)
                        P.dma("sync", o_dma(sg[:, 1, 0:tn], T["sg_b"][c, :, p0 + tt * tn:p0 + (tt + 1) * tn]), SG, True)
                        for kc in range(16):
                            P.op("tensor", o_mm(ya[:, 0:tn], wa[:, kc, cb * 128:(cb + 1) * 128], gta[:, kc, ts],
                                                kc == 0, kc == 15), reads=(WA, GTAS[kc // 4]), writes=(YA,), inc=(kc == 15))
                        for kc in range(16):
                            P.op("tensor", o_mm(yb[:, 0:tn], wb[:, kc, cb * 128:(cb + 1) * 128], gtb[:, kc, ts],
                                                kc == 0, kc == 15), reads=(WB, GTBS[kc // 4]), writes=(YB,), inc=(kc == 15))
                        P.op("vector", o_tt(t1[:, 0:tn], ya[:, 0:tn], sg[:, 0, 0:tn], ALU.mult), reads=(YA, SG), writes=(T1,))
                        P.op("vector", o_tt(t2[:, 0:tn], yb[:, 0:tn], sg[:, 1, 0:tn], ALU.mult), reads=(YB, SG), writes=(T2,))
                        P.op("gpsimd", o_tt(mT[:, c, ts], t1[:, 0:tn], t2[:, 0:tn], ALU.add), reads=(T1, T2), writes=(MT,))

        def step2(P, sb, ps, p0=p0, pn=pn):
            mT, MT = hold["mT"]
            r = 0 if p0 < TL else 1
            wvo = K.w_out[l].rearrange("(k p) c -> p k c", p=128)
            wo, _ = sb("wo", [128, 4, 16, 512], BF16)
            WOS = [Buf(f"wo{g}") for g in range(4)]
            for g in range(4):
                P.dma("gpsimd", o_dma(wo[:, g], wvo[:, :, g * 512:(g + 1) * 512]), WOS[g], True)
            gp, GP = sb("gp", [128, D], F32)
            gpo, GPO = sb("gpo", [128, D], F32)
            P.dma("sync", o_dma(gp[:], K.mod[l][r, 2 * D:3 * D].partition_broadcast(128)), GP, True)
            P.dma("sync", o_dma(gpo[:], K.g_post[l].partition_broadcast(128)), GPO, True)
            P.op("vector", o_tt(gp[:], gp[:], gpo[:], ALU.mult), reads=(GP, GPO), writes=(GP,))
            xts = [sb("xr", [128, D], F32) for _ in range(2)]
            ys = [sb("yo", [128, D], F32) for _ in range(2)]
            junk, JUNK = sb("junk", [128, 512], BF16)
            st_, ST = sb("stc", [128, 8], F32)
            pY = [ps("pY", [128, 512], F32) for _ in range(8)]
            for t in range(pn // 128):
                xt, XT = xts[t % 2]
                y, Y = ys[t % 2]
                tok = p0 + t * 128
                P.dma("sync", o_dma(xt[:], xsrc[tok:tok + 128, :]), XT, True)
                banks = [pY[(t % 2) * 4 + g] for g in range(4)]
                for g in range(4):
                    b_, B_ = banks[g]
                    for kc in range(16):
                        P.op("tensor", o_mm(b_[:], mT[:, kc, t * 128:(t + 1) * 128], wo[:, g, kc, :], kc == 0, kc == 15),
                             reads=(MT, WOS[g]), writes=(B_,), inc=(kc == 15))
                    P.op("scalar", o_act(junk[:], b_[:], AF.Square, accum_out=st_[:, g:g + 1]),
                         reads=(B_,), writes=(JUNK, ST))
                P.op("vector", o_rsum(st_[:, 4:5], st_[:, 0:4]), reads=(ST,), writes=(ST,))
                P.op("vector", o_ts(st_[:, 5:6], st_[:, 4:5], 1.0 / D, ALU.mult, EPS, ALU.add), reads=(ST,), writes=(ST,))
                P.op("scalar", o_act(st_[:, 5:6], st_[:, 5:6], AF.Ln), reads=(ST,), writes=(ST,))
                P.op("scalar", o_act(st_[:, 6:7], st_[:, 5:6], AF.Exp, scale=-0.5), reads=(ST,), writes=(ST,))
                for g in range(4):
                    b_, B_ = banks[g]
                    cs = slice(g * 512, (g + 1) * 512)
                    P.op("vector", o_stt(y[:, cs], b_[:], st_[:, 6:7], gp[:, cs], ALU.mult, ALU.mult),
                         reads=(B_, ST, GP), writes=(Y,))
                P.op("gpsimd", o_tt(y[:], y[:], xt[:], ALU.add), reads=(Y, XT), writes=(Y,))
                P.dma("sync", o_dma(xdst[tok:tok + 128, :], y[:]), Y, False)

        with ExitStack() as st0:
            K.pid += 1
            mt = st0.enter_context(K.nc.sbuf_tensor(f"mT_{K.pid}", [128, 16, pn], BF16))
            hold["mT"] = (mt, Buf("mT"))
            run_phase(K, step1)
            run_phase(K, step2)


SCR = {
    "gth": ([GROWS, 2048], BF16), "gat": ([NGCH, 4, GCH, 2048], BF16),
    "kvc_k_a": ([4, 128, CTX], BF16), "kvc_k_b": ([16, 128, CTX], BF16),
    "kvc_v_a": ([CTX, 512], BF16), "kvc_v_b": ([CTX, 2048], BF16),
    "qt_a": ([16, 128, NT], BF16), "qt_b": ([16, 128, NT], BF16),
    "z_a": ([NT, D], BF16), "z_b": ([NT, D], BF16),
    "sg_a": ([16, 128, NT], BF16), "sg_b": ([16, 128, NT], BF16),
    "gt_a": ([16, 128, NT], BF16), "gt_b": ([16, 128, NT], BF16),
}
WSHAPES = {"w_ada": [D, 3 * D], "b_ada": [3 * D], "g_pre": [D], "g_post": [D], "w_in": [D, IN_COLS],
           "sink": [16], "lam_qk": [4, 128], "g_subln": [256], "w_proj_a": [D, D], "w_proj_b": [D, D],
           "w_out": [D, D]}


def new_ctx():
    K = Ctx()
    K.nc = bass.Bass("TRN2", target_bir_lowering=False)
    K.pid = 0
    K.marks = []
    K.ecount = {e: 0 for e in CENGS}
    K.waited = {e: {} for e in ENGS}
    K.T = {0: {}, 1: {}}
    return K


def declare(K, name, shape, dt, kind):
    return K.nc.dram_tensor(name, list(shape), dt, kind=kind).ap()


def declare_weights(K, layers, names):
    for n in names:
        if not hasattr(K, n):
            setattr(K, n, {})
        for l in layers:
            getattr(K, n)[l] = declare(K, f"{n}{l}", WSHAPES[n], F32, "ExternalInput")


def build_program():
    K = new_ctx()
    K.mod = {}
    K.xall = declare(K, "xall", [NT, D], F32, "ExternalInput")
    K.cc = declare(K, "cc", [2, D], F32, "ExternalInput")
    K.rowbase = declare(K, "rowbase", [64, 1], F32, "ExternalInput")
    K.mvalid = declare(K, "mvalid", [10], F32, "ExternalInput")
    K.ropeC = declare(K, "ropeC", [128, TL], F32, "Internal")
    K.ropeS = declare(K, "ropeS", [128, TL], F32, "Internal")
    declare_weights(K, range(DEPTH), list(WSHAPES))
    for ll in range(DEPTH):
        K.mod[ll] = declare(K, f"mod{ll}", [2, 3 * D], F32, "Internal")
        for n in SCR:
            K.T[ll][n] = declare(K, f"{n}{ll}", SCR[n][0], SCR[n][1], "Internal")
    K.x1 = declare(K, "x1", [NT, D], F32, "Internal")
    K.xout = declare(K, "xout", [TL, D], F32, "ExternalOutput")
    with ExitStack() as st:
        K.esets = {ll: {e: st.enter_context(K.nc.semaphore(f"es{ll}_{e}")) for e in CENGS} for ll in range(DEPTH)}
        K.ccsem = {ll: st.enter_context(K.nc.semaphore(f"cc{ll}")) for ll in range(DEPTH)}
        K.dslots = [[st.enter_context(K.nc.semaphore(f"ds{i}")), 0] for i in range(40)]
        use_layer_sems(K, 0)
        phase_rope(K)
        for ll in range(DEPTH):
            phase_adaln(K, ll)
        for ll in range(DEPTH):
            lst = ll == DEPTH - 1
            if ll > 0:
                use_layer_sems(K, ll)
            xsrc = K.xall if ll == 0 else K.x1
            xdst = K.xout if lst else K.x1
            phase_A(K, ll, lst, xsrc, gather=True)
            phase_attnA(K, ll, lst)
            phase_attnB(K, ll, lst)
            phase_C(K, ll, lst, xsrc, xdst)
    return K.nc


_PROG = {}


def kernel(x, c, ctx, c_ctx, w_ada, b_ada, g_pre, g_post, w_in, sink, lam_qk, g_subln,
           w_proj_a, w_proj_b, w_out):
    f = lambda a: np.ascontiguousarray(np.asarray(a, dtype=np.float32))
    x, c, ctx, c_ctx = f(x), f(c), f(ctx), f(c_ctx)
    W = {"w_ada": f(w_ada), "b_ada": f(b_ada), "g_pre": f(g_pre), "g_post": f(g_post), "w_in": f(w_in),
         "sink": f(sink), "lam_qk": f(lam_qk), "g_subln": f(g_subln), "w_proj_a": f(w_proj_a),
         "w_proj_b": f(w_proj_b), "w_out": f(w_out)}
    cores = list(range(8))
    if "nc" not in _PROG:
        _PROG["nc"] = build_program()
    ins = []
    for r in cores:
        b, s = r // 4, r % 4
        rowbase = np.zeros((64, 1), np.float32)
        rowbase[:32] = s * (TL // 64)
        mvalid = np.zeros((10,), np.float32)
        for q in range(4):
            mvalid[q] = 1.0 if q == s - 1 else 0.0
            mvalid[4 + q] = 1.0 if q == s + 1 else 0.0
        mvalid[8:] = 1.0
        d = {"xall": np.concatenate([x[b, s * TL:(s + 1) * TL], ctx[b]], 0),
             "cc": np.stack([c[b], c_ctx], 0), "rowbase": rowbase, "mvalid": mvalid}
        for n in WSHAPES:
            for l in range(DEPTH):
                d[f"{n}{l}"] = W[n][l]
        ins.append(d)
    res = run_bass_kernel_spmd(_PROG["nc"], ins, core_ids=cores).results
    out = np.zeros((2, SEQ, D), np.float32)
    for r in cores:
        out[r // 4, (r % 4) * TL:(r % 4 + 1) * TL] = res[r]["xout"]
    return out
```

```python
import math
from contextlib import ExitStack

import numpy as np
import concourse.bass as bass
import concourse.mybir as mybir
from concourse.bass_utils import run_bass_kernel_spmd

F32 = mybir.dt.float32
BF16 = mybir.dt.bfloat16
AF = mybir.ActivationFunctionType
ALU = mybir.AluOpType
AX = mybir.AxisListType

D = 2048
SEQ = 8192
CTX = 256
TL = 2048
NT = TL + CTX
NTILE = NT // 128
DEPTH = 2
IN_COLS = 17408
EPS = 1e-6
SCALE = 128 ** -0.5
GROWS = 5120
R_KTA, R_KTB, R_VA, R_VB = 0, 512, 2560, 3072

CENGS = ("tensor", "vector", "scalar", "gpsimd")
ENGS = CENGS + ("sync",)
GCH = 256
NGCH = GROWS // GCH


def gat_rows(gat, r, row0, n):
    assert row0 // GCH == (row0 + n - 1) // GCH
    return gat[row0 // GCH, r, row0 % GCH:row0 % GCH + n, :]


class Buf:
    __slots__ = ("name", "lw", "rd", "dsem", "dtotal", "dsynced", "dbase", "slot", "_stored")

    def __init__(self, name):
        self.name = name
        self.lw = None
        self.rd = {}
        self.dsem = None
        self.dtotal = 0
        self.dsynced = {}
        self.dbase = 0
        self.slot = None


class Plan:
    nsem = 0

    def __init__(self, K, new_sem):
        self.nc = K.nc
        self.new_sem = new_sem
        self.esem = K.esem
        self.ecount = K.ecount
        self.waited = K.waited
        self.ops = {e: [] for e in ENGS}
        self.dbufs = []
        self.noinc_pending = {e: False for e in CENGS}

    def _wait_eng(self, E, waits, F, seq):
        w = self.waited[E]
        if w.get(F, 0) >= seq:
            return
        w[F] = seq
        waits.append((self.esem[F], seq))

    def _wait_dma(self, E, waits, b):
        if b.dtotal > b.dsynced.get(E, b.dbase):
            b.dsynced[E] = b.dtotal
            waits.append((b.dsem, b.dtotal))

    def _deps(self, E, reads, writes, dma_load=False):
        waits = []
        for b in reads:
            if b.lw is not None:
                self._wait_eng(E, waits, b.lw[0], b.lw[1])
            self._wait_dma(E, waits, b)
        for b in writes:
            if b.lw is not None and b.lw[0] != E:
                self._wait_eng(E, waits, b.lw[0], b.lw[1])
            for F, s in b.rd.items():
                if F != E:
                    self._wait_eng(E, waits, F, s)
            if not dma_load:
                self._wait_dma(E, waits, b)
        return waits

    def op(self, E, fn, reads=(), writes=(), inc=True):
        waits = self._deps(E, reads, writes)
        if inc:
            self.ecount[E] += 1
            seq = self.ecount[E]
        else:
            seq = self.ecount[E] + 1
        self.noinc_pending[E] = not inc
        for b in reads:
            b.rd[E] = seq
        for b in writes:
            b.lw = (E, seq)
            b.rd = {}
        self.ops[E].append((waits, fn, (self.esem[E], 1) if inc else None))

    def dma(self, Q, fn, buf, load):
        if buf.dsem is None:
            buf.slot = self.new_sem(buf.name)
            buf.dsem = buf.slot[0]
            buf.dbase = buf.dtotal = buf.slot[1]
            self.dbufs.append(buf)
        waits = self._deps(Q, () if load else (buf,), (buf,) if load else (), dma_load=load)
        if load:
            assert not getattr(buf, "_stored", False)
            buf.lw = None
            buf.rd = {}
        else:
            buf._stored = True
        buf.dtotal += 16
        self.ops[Q].append((waits, fn, (buf.dsem, 16)))

    def finish(self, Q="sync"):
        assert not any(self.noinc_pending.values()), self.noinc_pending
        waits = []
        for b in self.dbufs:
            self._wait_dma(Q, waits, b)
            b.slot[1] = b.dtotal
        if waits:
            self.ops[Q].append((waits, None, None))

    def replay(self, block):
        def mk(E):
            ops = self.ops[E]

            def body(eng):
                for waits, fn, inc in ops:
                    for sem, val in waits:
                        eng.wait_ge(sem, val)
                    if fn is not None:
                        ins = fn(eng)
                        if inc is not None:
                            ins.then_inc(inc[0], inc[1])
            return body
        for E in ENGS:
            if self.ops[E]:
                getattr(block, E)(mk(E))


class Ctx:
    pass


def run_phase(K, body):
    nslot = [0]

    def new_slot(name):
        sl = K.dslots[nslot[0]]
        nslot[0] += 1
        return sl
    with ExitStack() as st:
        K.pid += 1
        P = Plan(K, new_slot)
        cnt = [0]

        def sb(name, shape, dt):
            cnt[0] += 1
            t = st.enter_context(K.nc.sbuf_tensor(f"{name}_{K.pid}_{cnt[0]}", list(shape), dt))
            return t, Buf(name)

        def ps(name, shape, dt):
            cnt[0] += 1
            t = st.enter_context(K.nc.psum_tensor(f"{name}_{K.pid}_{cnt[0]}", list(shape), dt))
            return t, Buf(name)
        body(P, sb, ps)
        P.finish()
        K.marks.append((getattr(body, "__qualname__", "?"), dict(K.ecount)))
        with K.nc.Block() as block:
            P.replay(block)


def use_layer_sems(K, l):
    K.esem = K.esets[l]
    K.ecount = {e: 0 for e in CENGS}
    K.waited = {e: {} for e in ENGS}


def o_mm(out, lhsT, rhs, start, stop):
    return lambda e: e.matmul(out, lhsT=lhsT, rhs=rhs, start=start, stop=stop)


def o_tr(out, in_, ident):
    return lambda e: e.transpose(out, in_, ident)


def o_act(out, in_, func, scale=None, bias=None, accum_out=None):
    kw = {}
    if scale is not None:
        kw["scale"] = scale
    if bias is not None:
        kw["bias"] = bias
    if accum_out is not None:
        kw["accum_out"] = accum_out
    return lambda e: e.activation(out=out, in_=in_, func=func, **kw)


def o_tt(out, in0, in1, op):
    return lambda e: e.tensor_tensor(out=out, in0=in0, in1=in1, op=op)


def o_ts(out, in0, s1, op0, s2=None, op1=None):
    if op1 is None:
        return lambda e: e.tensor_scalar(out=out, in0=in0, scalar1=s1, scalar2=None, op0=op0)
    return lambda e: e.tensor_scalar(out=out, in0=in0, scalar1=s1, scalar2=s2, op0=op0, op1=op1)


def o_stt(out, in0, scalar, in1, op0, op1):
    return lambda e: e.scalar_tensor_tensor(out=out, in0=in0, scalar=scalar, in1=in1, op0=op0, op1=op1)


def o_copy(out, in_):
    return lambda e: e.tensor_copy(out=out, in_=in_)


def o_acopy(out, in_):
    return lambda e: e.copy(out=out, in_=in_)


def o_memset(ap, v):
    return lambda e: e.memset(ap, v)


def o_recip(out, in_):
    return lambda e: e.reciprocal(out=out, in_=in_)


def o_rsum(out, in_):
    return lambda e: e.reduce_sum(out=out, in_=in_, axis=AX.X)


def o_dma(out, in_, slow=False):
    if slow:
        return lambda e: e.dma_start(out=out, in_=in_, allow_slow_non_contiguous=True)
    return lambda e: e.dma_start(out=out, in_=in_)


def o_gather(gth, gat, i):
    return lambda e: e.collective_compute("AllGather", ALU.bypass, replica_groups=[[0, 1, 2, 3], [4, 5, 6, 7]],
                                          ins=[gth[i * GCH:(i + 1) * GCH, :]],
                                          outs=[gat[i].rearrange("r g c -> (r g) c")])


def fm_vec(ap1d):
    return ap1d.rearrange("(k p) -> p k", p=128)


def make_ident(P, sb, dt):
    idf, IDF = sb("identf", [128, 128], F32)
    P.op("gpsimd", o_memset(idf[:], 1.0), writes=(IDF,))
    P.op("gpsimd", lambda e: e.affine_select(out=idf[:], in_=idf[:], pattern=[[-1, 128]],
                                              compare_op=ALU.is_equal, fill=0.0, base=0,
                                              channel_multiplier=1), reads=(IDF,), writes=(IDF,))
    if dt == F32:
        return idf, IDF
    idb, IDB = sb("identb", [128, 128], BF16)
    P.op("vector", o_copy(idb[:], idf[:]), reads=(IDF,), writes=(IDB,))
    return idb, IDB


def phase_adaln(K, l):
    def body(P, sb, ps):
        ccT, CCT = sb("ccT", [128, 2, 16], F32)
        for r in range(2):
            P.dma("sync", o_dma(ccT[:, r, :], fm_vec(K.cc[r]), slow=True), CCT, True)
        P.op("scalar", o_act(ccT[:], ccT[:], AF.Silu), reads=(CCT,), writes=(CCT,))
        brow, BROW = sb("brow", [2, 3 * D], F32)
        P.dma("sync", o_dma(brow[:], K.b_ada[l].partition_broadcast(2)), BROW, True)
        mrow, MROW = sb("mrow", [2, 3 * D], F32)
        wv = K.w_ada[l].rearrange("(k p) c -> p k c", p=128)
        wts = [sb("wada", [128, 16, 512], F32) for _ in range(2)]
        pss = [ps("pada", [128, 512], F32) for _ in range(2)]
        for ct in range(12):
            wt, WT = wts[ct % 2]
            pt, PT = pss[ct % 2]
            cs = slice(ct * 512, (ct + 1) * 512)
            P.dma("sync", o_dma(wt[:], wv[:, :, cs]), WT, True)
            for kc in range(16):
                P.op("tensor", o_mm(pt[0:2, :], ccT[:, :, kc], wt[:, kc, :], kc == 0, kc == 15),
                     reads=(CCT, WT), writes=(PT,), inc=(kc == 15))
            P.op("vector", o_tt(mrow[:, cs], pt[0:2, :], brow[:, cs], ALU.add),
                 reads=(PT, BROW), writes=(MROW,))
        P.dma("sync", o_dma(K.mod[l], mrow[:]), MROW, False)
    run_phase(K, body)


def phase_rope(K):
    PI = math.pi

    def body(P, sb, ps):
        pos, POS = sb("pos", [64, TL], F32)
        fidx, FIDX = sb("fidx", [64, 1], F32)
        rb, RB = sb("rowb", [64, 1], F32)
        P.dma("sync", o_dma(rb[:], K.rowbase), RB, True)
        P.op("gpsimd", lambda e: e.iota(pos[0:32, :], pattern=[[1, TL // 64], [0, 64]], base=0, channel_multiplier=0,
                                        allow_small_or_imprecise_dtypes=True), writes=(POS,))
        P.op("gpsimd", lambda e: e.iota(pos[32:64, :], pattern=[[0, TL // 64], [1, 64]], base=0, channel_multiplier=0,
                                        allow_small_or_imprecise_dtypes=True), writes=(POS,))
        P.op("gpsimd", lambda e: e.iota(fidx[0:32, :], pattern=[[0, 1]], base=0, channel_multiplier=1,
                                        allow_small_or_imprecise_dtypes=True), writes=(FIDX,))
        P.op("gpsimd", lambda e: e.iota(fidx[32:64, :], pattern=[[0, 1]], base=0, channel_multiplier=1,
                                        allow_small_or_imprecise_dtypes=True), writes=(FIDX,))
        P.op("scalar", o_act(fidx[:], fidx[:], AF.Exp, scale=-math.log(10000.0) / 32.0), reads=(FIDX,), writes=(FIDX,))
        ang, ANG = sb("ang", [64, TL], F32)
        P.op("vector", o_ts(ang[:], pos[:], rb[:, 0:1], ALU.add, fidx[:, 0:1], ALU.mult),
             reads=(POS, RB, FIDX), writes=(ANG,))
        kf, KF = sb("kf", [64, TL], F32)
        ki, KI = sb("ki", [64, TL], mybir.dt.int32)
        red, RED = sb("red", [64, TL], F32)
        msk, MSK = sb("msk", [64, TL], F32)
        outc, OUTC = sb("outc", [64, TL], F32)
        outs, OUTS = sb("outs", [64, TL], F32)
        outn, OUTN = sb("outn", [64, TL], F32)

        def reduce_sin(shift, out, OUT):
            P.op("vector", o_ts(kf[:], ang[:], shift, ALU.add, 1.0 / (2 * PI), ALU.mult), reads=(ANG,), writes=(KF,))
            P.op("vector", o_copy(ki[:], kf[:]), reads=(KF,), writes=(KI,))
            P.op("vector", o_copy(kf[:], ki[:]), reads=(KI,), writes=(KF,))
            P.op("vector", o_stt(red[:], kf[:], -2 * PI, ang[:], ALU.mult, ALU.add), reads=(KF, ANG), writes=(RED,))
            if shift != 0.0:
                P.op("vector", o_ts(red[:], red[:], shift, ALU.add), reads=(RED,), writes=(RED,))
            P.op("vector", o_ts(msk[:], red[:], PI, ALU.is_gt), reads=(RED,), writes=(MSK,))
            P.op("vector", o_stt(red[:], msk[:], -2 * PI, red[:], ALU.mult, ALU.add), reads=(MSK, RED), writes=(RED,))
            P.op("vector", o_ts(msk[:], red[:], -PI, ALU.is_lt), reads=(RED,), writes=(MSK,))
            P.op("vector", o_stt(red[:], msk[:], 2 * PI, red[:], ALU.mult, ALU.add), reads=(MSK, RED), writes=(RED,))
            P.op("scalar", o_act(out[:], red[:], AF.Sin), reads=(RED,), writes=(OUT,))
        reduce_sin(0.0, outs, OUTS)
        reduce_sin(PI / 2, outc, OUTC)
        P.op("vector", o_ts(outn[:], outs[:], -1.0, ALU.mult), reads=(OUTS,), writes=(OUTN,))
        P.dma("sync", o_dma(K.ropeC[0:64, :], outc[:]), OUTC, False)
        P.dma("sync", o_dma(K.ropeC[64:128, :], outc[:]), OUTC, False)
        P.dma("sync", o_dma(K.ropeS[0:64, :], outn[:]), OUTN, False)
        P.dma("sync", o_dma(K.ropeS[64:128, :], outs[:]), OUTS, False)
    run_phase(K, body)


FAMS = [("k_a", 0, 512, "rope"), ("v_a", 512, 512, "v"), ("k_b", 1024, 2048, "rope"),
        ("v_b", 3072, 2048, "v"), ("q_a", 5120, 2048, "rope"), ("z_a", 7168, 2048, "z"),
        ("q_b", 9216, 2048, "rope"), ("z_b", 11264, 2048, "z"), ("g_a", 13312, 2048, "g"),
        ("g_b", 15360, 2048, "g")]


def phase_A(K, l, last, xsrc, fams=None, gather=False):
    T = K.T[l]

    def body(P, sb, ps):
        hxT, _ = sb("hxT", [128, 16, NT], BF16)
        HXA = [Buf(f"hxTa{i}") for i in range(5)]
        HXV = [Buf(f"hxTv{i}") for i in range(5)]
        idf, IDF = make_ident(P, sb, F32)
        xts = [sb("xt", [128, D], F32) for _ in range(2)]
        for t in range(2):
            P.dma("sync", o_dma(xts[t][0][:], xsrc[t * 128:(t + 1) * 128, :]), xts[t][1], True)
        cst, CST = sb("cst", [128, 5, 16], F32)
        P.dma("sync", o_dma(cst[:, 0, :], fm_vec(K.g_pre[l]), slow=True), CST, True)
        for r in range(2):
            P.dma("sync", o_dma(cst[:, 1 + 2 * r, :], fm_vec(K.mod[l][r, D:2 * D]), slow=True), CST, True)
            P.dma("sync", o_dma(cst[:, 2 + 2 * r, :], fm_vec(K.mod[l][r, 0:D]), slow=True), CST, True)
        for r in range(2):
            P.op("vector", o_stt(cst[:, 1 + 2 * r, :], cst[:, 1 + 2 * r, :], 1.0, cst[:, 0, :], ALU.add, ALU.mult),
                 reads=(CST,), writes=(CST,))
        ropeC, ROPEC = sb("ropeC", [128, TL], F32)
        ropeS, ROPES = sb("ropeS", [128, TL], F32)
        junk, JUNK = sb("junk", [128, D], BF16)
        stat, STAT = sb("stat", [128, 3, NTILE], F32)
        ptr = [ps("ptr", [128, 512], F32) for _ in range(4)]
        nb = 0
        for t in range(NTILE):
            xt, XT = xts[t % 2]
            r = 0 if t < 16 else 1
            if t >= 2:
                P.dma("sync", o_dma(xt[:], xsrc[t * 128:(t + 1) * 128, :]), XT, True)
            if t == 1:
                P.dma("sync", o_dma(ropeC[:], K.ropeC), ROPEC, True)
                P.dma("sync", o_dma(ropeS[:], K.ropeS), ROPES, True)
            P.op("scalar", o_act(junk[:], xt[:], AF.Square, accum_out=stat[:, 0, t:t + 1]),
                 reads=(XT,), writes=(JUNK, STAT))
            P.op("vector", o_ts(stat[:, 1, t:t + 1], stat[:, 0, t:t + 1], 1.0 / D, ALU.mult, EPS, ALU.add),
                 reads=(STAT,), writes=(STAT,))
            P.op("scalar", o_act(stat[:, 1, t:t + 1], stat[:, 1, t:t + 1], AF.Ln), reads=(STAT,), writes=(STAT,))
            P.op("scalar", o_act(stat[:, 2, t:t + 1], stat[:, 1, t:t + 1], AF.Exp, scale=-0.5),
                 reads=(STAT,), writes=(STAT,))
            P.op("vector", o_ts(xt[:], xt[:], stat[:, 2, t:t + 1], ALU.mult), reads=(XT, STAT), writes=(XT,))
            for kq in range(4):
                banks = (ptr[nb % 4], ptr[(nb + 1) % 4])
                nb += 2
                for j in range(4):
                    kc = kq * 4 + j
                    pt, PT = banks[j % 2]
                    P.op("tensor", o_tr(pt[:, (j // 2) * 128:(j // 2 + 1) * 128], xt[:, kc * 128:(kc + 1) * 128], idf[:]),
                         reads=(XT, IDF), writes=(PT,), inc=(j >= 2))
                for j in range(4):
                    kc = kq * 4 + j
                    pt, PT = banks[j % 2]
                    src = pt[:, (j // 2) * 128:(j // 2 + 1) * 128]
                    if j % 2 == 0:
                        P.op("scalar", o_act(hxT[:, kc, t * 128:(t + 1) * 128], src,
                                             AF.Identity, scale=cst[:, 1 + 2 * r, kc:kc + 1],
                                             bias=cst[:, 2 + 2 * r, kc:kc + 1]),
                             reads=(PT, CST), writes=(HXA[t // 4],))
                    else:
                        P.op("vector", o_ts(hxT[:, kc, t * 128:(t + 1) * 128], src,
                                            cst[:, 1 + 2 * r, kc:kc + 1], ALU.mult,
                                            cst[:, 2 + 2 * r, kc:kc + 1], ALU.add),
                             reads=(PT, CST), writes=(HXV[t // 4],))
        wv = K.w_in[l].rearrange("(k p) c -> p k c", p=128)
        wts = [sb("win", [128, 16, 512], BF16) for _ in range(3)]
        pacc = [ps("pacc", [128, 512], F32) for _ in range(4)]
        stg = [sb("stg", [128, 512], BF16) for _ in range(4)]
        tm1 = [sb("tm1", [128, 512], F32) for _ in range(2)]
        tm2 = [sb("tm2", [128, 512], F32) for _ in range(2)]
        cnt = {"w": 0, "p": 0, "s": 0, "t": 0}
        tts = [(0, 512), (512, 512), (1024, 512), (1536, 512), (2048, 256)]

        def fm_dest(name, c, t0, n):
            if name in ("k_a", "k_b"):
                if t0 < TL:
                    r0 = (R_KTA if name == "k_a" else R_KTB) + c * 128
                    return T["gth"][r0:r0 + 128, t0:t0 + n]
                return T["kvc_" + name][c, :, t0 - TL:t0 - TL + n]
            tn = {"q_a": "qt_a", "q_b": "qt_b", "g_a": "sg_a", "g_b": "sg_b"}[name]
            return T[tn][c, :, t0:t0 + n]

        def tm_dest(name, g, t):
            if name in ("v_a", "v_b"):
                if t < 16:
                    if name == "v_a":
                        return bass.AP(T["gth"].tensor, R_VA * 2048 + t * 128 * 512, [[512, 128], [1, 512]])
                    return T["gth"][R_VB + t * 128:R_VB + (t + 1) * 128, g * 512:(g + 1) * 512]
                return T["kvc_" + name][(t - 16) * 128:(t - 15) * 128, g * 512:(g + 1) * 512]
            return T[name][t * 128:(t + 1) * 128, g * 512:(g + 1) * 512]

        groups = [(name, c0, kind, g) for (name, c0, ncols, kind) in FAMS
                  if fams is None or name in fams for g in range(ncols // 512)]

        def wload(i):
            if i < len(groups):
                name_, c0_, _, g_ = groups[i]
                wt_, WT_ = wts[i % 3]
                P.dma("gpsimd", o_dma(wt_[:], wv[:, :, c0_ + g_ * 512:c0_ + (g_ + 1) * 512]), WT_, True)
        wload(0)
        wload(1)
        kv_done = False
        for gi, (name, c0, kind, g) in enumerate(groups):
            if True:
                qpart = c0 >= 5120
                if qpart and not kv_done:
                    kv_done = True
                    if gather:
                        waits = []
                        for (_, SG_) in stg:
                            if SG_.dsem is not None:
                                P._wait_dma("gpsimd", waits, SG_)
                        P.ops["gpsimd"].append((waits, None, None))
                        for i in range(NGCH):
                            P.ops["gpsimd"].append(([], o_gather(T["gth"], T["gat"], i), (K.ccsem[l], 1)))
                wload(gi + 2)
                wt, WT = wts[gi % 3]
                if kind in ("rope", "g"):
                    for cb in range(4):
                        c = g * 4 + cb
                        for (t0, n) in tts:
                            if t0 >= TL and qpart and last:
                                continue
                            pt, PT = pacc[cnt["p"] % 4]
                            cnt["p"] += 1
                            for kc in range(16):
                                P.op("tensor", o_mm(pt[:, 0:n], wt[:, kc, cb * 128:(cb + 1) * 128],
                                                    hxT[:, kc, t0:t0 + n], kc == 0, kc == 15),
                                     reads=(WT, HXA[t0 // 512], HXV[t0 // 512]), writes=(PT,), inc=(kc == 15))
                            sg, SG = stg[cnt["s"] % 4]
                            cnt["s"] += 1
                            if kind == "g":
                                P.op("scalar", o_act(sg[:, 0:n], pt[:, 0:n], AF.Sigmoid), reads=(PT,), writes=(SG,))
                            elif t0 >= TL:
                                P.op("scalar", o_acopy(sg[:, 0:n], pt[:, 0:n]), reads=(PT,), writes=(SG,))
                            else:
                                t1, T1 = tm1[cnt["t"] % 2]
                                t2, T2 = tm2[cnt["t"] % 2]
                                cnt["t"] += 1
                                P.op("vector", o_tt(t1[:], pt[:], ropeC[:, t0:t0 + n], ALU.mult),
                                     reads=(PT, ROPEC), writes=(T1,))
                                P.op("vector", o_tt(t2[0:64, :], pt[64:128, :], ropeS[0:64, t0:t0 + n], ALU.mult),
                                     reads=(PT, ROPES), writes=(T2,))
                                P.op("vector", o_tt(t2[64:128, :], pt[0:64, :], ropeS[64:128, t0:t0 + n], ALU.mult),
                                     reads=(PT, ROPES), writes=(T2,))
                                P.op("gpsimd", o_tt(sg[:], t1[:], t2[:], ALU.add), reads=(T1, T2), writes=(SG,))
                            P.dma("sync", o_dma(fm_dest(name, c, t0, n), sg[:, 0:n]), SG, False)
                else:
                    for t in range(NTILE):
                        if t >= 16 and qpart and last:
                            continue
                        pt, PT = pacc[cnt["p"] % 4]
                        cnt["p"] += 1
                        for kc in range(16):
                            P.op("tensor", o_mm(pt[:], hxT[:, kc, t * 128:(t + 1) * 128], wt[:, kc, :],
                                                kc == 0, kc == 15), reads=(WT, HXA[t // 4], HXV[t // 4]), writes=(PT,), inc=(kc == 15))
                        sg, SG = stg[cnt["s"] % 4]
                        cnt["s"] += 1
                        if kind == "z":
                            P.op("scalar", o_act(sg[:], pt[:], AF.Silu), reads=(PT,), writes=(SG,))
                        else:
                            P.op("vector", o_copy(sg[:], pt[:]), reads=(PT,), writes=(SG,))
                        P.dma("sync", o_dma(tm_dest(name, g, t), sg[:]), SG, False)
        if gather:
            P.ops["gpsimd"].append(([(K.ccsem[l], NGCH)], None, None))
    run_phase(K, body)


def phase_attnA(K, l, last):
    T = K.T[l]

    def body(P, sb, ps):
        idb, IDB = make_ident(P, sb, BF16)
        NKC = TL + 8 * 128 + CTX
        kta, _ = sb("kta", [128, 4, NKC], BF16)
        NVB = 16 + 8 + 2
        va, _ = sb("va", [128, NVB, 4, 129], BF16)
        KTAS = [Buf(f"kta{h}") for h in range(4)]
        VAO = [Buf(f"vao{h}") for h in range(4)]
        VAC = Buf("vac")
        for h in range(4):
            P.op("vector", o_memset(va[:, 0:16, h, 128:129], 1.0), writes=(VAO[h],))
        P.op("vector", o_memset(va[:, 16:NVB, :, 128:129], 1.0), writes=(VAC,))
        gth, gat = T["gth"], T["gat"]

        def va_blk(r, j):
            row = R_VA + j * 32
            off = ((row // GCH * 4 + r) * GCH + row % GCH) * 2048
            return bass.AP(gat.tensor, off, [[512, 128], [128, 4], [1, 128]])
        for r in range(4):
            P.dma("sync", o_dma(va[:, 16 + r, :, 0:128], va_blk(r, 15)), VAC, True)
            P.dma("sync", o_dma(va[:, 20 + r, :, 0:128], va_blk(r, 0)), VAC, True)
        for cbk in range(2):
            P.dma("sync", o_dma(va[:, 24 + cbk, :, 0:128],
                                T["kvc_v_a"][cbk * 128:(cbk + 1) * 128, :].rearrange("p (h e) -> p h e", h=4)),
                  VAC, True)
        for h in range(4):
            P.dma("sync", o_dma(kta[:, h, 0:TL], gth[R_KTA + h * 128:R_KTA + (h + 1) * 128, :]), KTAS[h], True)
            for r in range(4):
                src = gat_rows(gat, r, R_KTA + h * 128, 128)
                P.dma("sync", o_dma(kta[:, h, TL + r * 128:TL + (r + 1) * 128], src[:, TL - 128:TL]), KTAS[h], True)
                P.dma("sync", o_dma(kta[:, h, TL + (4 + r) * 128:TL + (5 + r) * 128], src[:, 0:128]), KTAS[h], True)
            P.dma("sync", o_dma(kta[:, h, TL + 1024:TL + 1024 + CTX], T["kvc_k_a"][h, :, :]), KTAS[h], True)
            P.dma("sync", o_dma(va[:, 0:16, h, 0:128],
                                bass.AP(gth.tensor, R_VA * 2048 + h * 128, [[512, 128], [128 * 512, 16], [1, 128]])),
                  VAO[h], True)
        mask, MASK = sb("mask", [128, 10, 128], BF16)
        if True:
            mv, MV = sb("mv", [128, 10], F32)
            P.dma("sync", o_dma(mv[:], K.mvalid.partition_broadcast(128)), MV, True)
            trif, TRIF = sb("trif", [128, 2, 128], F32)
            P.op("gpsimd", o_memset(trif[:], 1.0), writes=(TRIF,))
            P.op("gpsimd", lambda e: e.affine_select(out=trif[:, 0, :], in_=trif[:, 0, :], pattern=[[-1, 128]],
                                                      compare_op=ALU.is_ge, fill=0.0, base=0, channel_multiplier=1),
                 reads=(TRIF,), writes=(TRIF,))
            P.op("gpsimd", lambda e: e.affine_select(out=trif[:, 1, :], in_=trif[:, 1, :], pattern=[[1, 128]],
                                                      compare_op=ALU.is_ge, fill=0.0, base=0, channel_multiplier=-1),
                 reads=(TRIF,), writes=(TRIF,))
            for m in range(10):
                tri = 0 if (m < 4 or m == 8) else 1
                P.op("vector", o_ts(mask[:, m, :], trif[:, tri, :], mv[:, m:m + 1], ALU.mult),
                     reads=(TRIF, MV), writes=(MASK,))
        mbias, MBIAS = sb("mbias", [128, 10, 512], BF16)
        for m in range(10):
            P.op("vector", o_ts(mbias[:, m, :].rearrange("p (g q) -> p g q", g=4),
                                mask[:, m:m + 1, :].broadcast_to([128, 4, 128]), -1.0, ALU.add, 30000.0, ALU.mult),
                 reads=(MASK,), writes=(MBIAS,))
        esk, ESK = sb("esk", [128, 16], F32)
        P.dma("sync", o_dma(esk[:], K.sink[l].partition_broadcast(128)), ESK, True)
        P.op("scalar", o_act(esk[:], esk[:], AF.Exp), reads=(ESK,), writes=(ESK,))

        qts = [sb("qta", [128, 2048], BF16) for _ in range(2)]
        zas = [sb("za", [128, D], BF16) for _ in range(2)]
        gs = [sb("ga", [128, D], BF16) for _ in range(2)]
        gts = [sb("gta", [128, 16, 128], BF16) for _ in range(2)]
        es = [sb("ea", [128, 512], BF16) for _ in range(5)]
        rr, RR = sb("rr", [128, 8], F32)
        pS = [ps("pSa", [128, 512], F32) for _ in range(3)]
        pO = [ps("pOa", [128, 512], F32) for _ in range(4)]
        oas = [sb("oas", [128, 4, 129], F32) for _ in range(2)]
        pT = [ps("pTa", [128, 8, 128], BF16) for _ in range(1)]
        cnt = {"s": 0, "e": 0, "r": 0}
        nqb = 16 if last else 18
        deferredA = []

        def tickA(force=False):
            for d in list(deferredA):
                d[0] -= 1
                if d[0] <= 0 or force:
                    deferredA.remove(d)
                    d[1]()

        def qz_loads(j):
            qt, QT = qts[j % 2]
            za, ZA = zas[j % 2]
            P.dma("sync", o_dma(qt[:].rearrange("p (c t) -> p c t", c=16),
                                T["qt_a"][:, :, j * 128:(j + 1) * 128].rearrange("c p t -> p c t")), QT, True)
            P.dma("sync", o_dma(za[:], T["z_a"][j * 128:(j + 1) * 128, :]), ZA, True)
        qz_loads(0)
        for j in range(nqb):
            qt, QT = qts[j % 2]
            za, ZA = zas[j % 2]
            g_, G_ = gs[j % 2]
            gt, GT = gts[j % 2]
            if j + 1 < nqb:
                qz_loads(j + 1)
            if j >= 16:
                kbl = [(TL + 1024, 24, None), (TL + 1024 + 128, 25, None)]
            else:
                kbl = []
                if j > 0:
                    kbl.append(((j - 1) * 128, j - 1, 8))
                else:
                    kbl += [(TL + r * 128, 16 + r, r) for r in range(4)]
                kbl.append((j * 128, j, None))
                if j < 15:
                    kbl.append(((j + 1) * 128, j + 1, 9))
                else:
                    kbl += [(TL + (4 + r) * 128, 20 + r, 4 + r) for r in range(4)]
                kbl += [(TL + 1024, 24, None), (TL + 1024 + 128, 25, None)]
            seq = [(h, ki, kc0, vb, mi) for h in range(4) for ki, (kc0, vb, mi) in enumerate(kbl)]

            def qk(item):
                h, ki, kc0, vb, mi = item
                s_, S_ = pS[cnt["s"] % 3]
                cnt["s"] += 1
                P.op("tensor", o_mm(s_[:], kta[:, h, kc0:kc0 + 128], qt[:, 4 * h * 128:(4 * h + 4) * 128],
                                    True, mi is None), reads=(KTAS[h], QT), writes=(S_,), inc=(mi is None))
                if mi is not None:
                    P.op("tensor", o_mm(s_[:], idb[:], mbias[:, mi, :], False, True),
                         reads=(IDB, MBIAS), writes=(S_,))
                return s_, S_
            ahead = [qk(seq[0]), qk(seq[1])]
            for idx, (h, ki, kc0, vb, mi) in enumerate(seq):
                tickA()
                s_, S_ = ahead.pop(0)
                e_, E_ = es[cnt["e"] % 5]
                cnt["e"] += 1
                P.op("scalar", o_act(e_[:], s_[:], AF.Exp, scale=SCALE), reads=(S_,), writes=(E_,))
                if idx + 2 < len(seq):
                    ahead.append(qk(seq[idx + 2]))
                for g in range(4):
                    o_, O_ = pO[g]
                    P.op("tensor", o_mm(o_[:, 0:129], e_[:, g * 128:(g + 1) * 128], va[:, vb, h, :],
                                        ki == 0, ki == len(kbl) - 1), reads=(E_, VAO[h], VAC), writes=(O_,), inc=(g == 3))
                if ki == len(kbl) - 1:
                    oa4, OA4 = oas[cnt["r"] % 2]
                    rc4 = rr[:, (cnt["r"] % 2) * 4:(cnt["r"] % 2) * 4 + 4]
                    cnt["r"] += 1
                    for g in range(4):
                        P.op("vector", o_copy(oa4[:, g, :], pO[g][0][:, 0:129]), reads=(pO[g][1],), writes=(OA4,))
                    P.op("vector", o_tt(rc4, oa4[:, :, 128], esk[:, 4 * h:4 * h + 4], ALU.add), reads=(OA4, ESK), writes=(RR,))
                    P.op("vector", o_recip(rc4, rc4), reads=(RR,), writes=(RR,))
                    for g in range(4):
                        hd = 4 * h + g
                        P.op("vector", o_stt(g_[:, hd * 128:(hd + 1) * 128], oa4[:, g, 0:128], rc4[:, g:g + 1],
                                             za[:, hd * 128:(hd + 1) * 128], ALU.mult, ALU.mult),
                             reads=(OA4, RR, ZA), writes=(G_,))
            def tail(g_=g_, G_=G_, gt=gt, GT=GT, j=j):
                for half in range(2):
                    t_, T_ = pT[0]
                    for c8 in range(8):
                        c = half * 8 + c8
                        P.op("tensor", o_tr(t_[:, c8, :], g_[:, c * 128:(c + 1) * 128], idb[:]),
                             reads=(G_, IDB), writes=(T_,), inc=(c8 == 7))
                    P.op("vector", o_copy(gt[:, half * 8:(half + 1) * 8, :], t_[:]), reads=(T_,), writes=(GT,))
                P.dma("sync", o_dma(T["gt_a"][:, :, j * 128:(j + 1) * 128].rearrange("c p t -> p c t"), gt[:]), GT, False)
            deferredA.append([10, tail])
        tickA(force=True)
    run_phase(K, body)


def phase_attnB(K, l, last, heads=range(8)):
    T = K.T[l]
    lam_init = 0.8 - 0.6 * math.exp(-0.3 * l)
    NK = CTX + SEQ
    NKB = NK // 128

    def body(P, sb, ps):
        idb, IDB = make_ident(P, sb, BF16)
        lq, LQ = sb("lq", [128, 4, 128], F32)
        P.dma("sync", o_dma(lq[:], K.lam_qk[l].partition_broadcast(128)), LQ, True)
        lsc, LSC = sb("lsc", [128, 8], F32)
        ljk, LJK = sb("ljk", [128, 2, 128], F32)
        for i in range(2):
            P.op("vector", o_tt(ljk[:, i, :], lq[:, 2 * i, :], lq[:, 2 * i + 1, :], ALU.mult), reads=(LQ,), writes=(LJK,))
            P.op("vector", o_rsum(lsc[:, i:i + 1], ljk[:, i, :]), reads=(LJK,), writes=(LSC,))
        P.op("scalar", o_act(lsc[:, 0:2], lsc[:, 0:2], AF.Exp), reads=(LSC,), writes=(LSC,))
        P.op("vector", o_tt(lsc[:, 2:3], lsc[:, 0:1], lsc[:, 1:2], ALU.subtract), reads=(LSC,), writes=(LSC,))
        P.op("vector", o_ts(lsc[:, 3:4], lsc[:, 2:3], lam_init, ALU.add), reads=(LSC,), writes=(LSC,))
        lam = lsc[:, 3:4]
        gsub, GSUB = sb("gsub", [128, 256], F32)
        P.dma("sync", o_dma(gsub[:], K.g_subln[l].partition_broadcast(128)), GSUB, True)
        P.op("vector", o_ts(gsub[:], gsub[:], 1.0 - lam_init, ALU.mult), reads=(GSUB,), writes=(GSUB,))

        kts = [sb("ktb", [128, 2, NK], BF16) for _ in range(2)]
        vbs = [sb("vb", [128, NKB, 257], BF16) for _ in range(2)]
        qbs = [sb("qtb", [128, 2, NT], BF16) for _ in range(2)]
        zbs = [sb("zb", [128, 2, 256], BF16) for _ in range(2)]
        es = [sb("eb", [128, 512], BF16) for _ in range(6)]
        sm, SM = sb("smb", [128, 16], F32)
        tmp, TMP = sb("tmpb", [128, 256], F32)
        ot, OT = sb("ob", [128, 256], F32)
        sq, SQ = sb("sqb", [128, 256], F32)
        gz, GZ = sb("gzb", [128, 256], F32)
        gbs = [sb("gb", [128, 256], BF16) for _ in range(4)]
        gtb = [sb("gtb", [128, 2, 256], BF16) for _ in range(2)]
        pS = [ps("pSb", [128, 512], F32) for _ in range(3)]
        pO = [ps("pOb", [128, 512], F32) for _ in range(4)]
        pT, PT = ps("pTb", [128, 2, 256], BF16)
        gat = T["gat"]
        cnt = {"s": 0, "e": 0, "z": 0, "g": 0, "t": 0}
        nqt = 8 if last else 9
        osb = [sb("osb", [128, 257], F32) for _ in range(4)]
        mhalf, MHALF = sb("mhalf", [128, 1], F32)
        P.op("gpsimd", o_memset(mhalf[:], -0.5), writes=(MHALF,))
        heads_l = list(heads)

        KTP = [[Buf(f"ktb{i}_{p}") for p in range(5)] for i in range(2)]
        VBP = [[Buf(f"vbb{i}_{p}") for p in range(5)] for i in range(2)]
        for i in range(2):
            P.op("vector", o_memset(vbs[i][0][:, 0:2, 256:257], 1.0), writes=(VBP[i][0],))
            for r in range(4):
                P.op("vector", o_memset(vbs[i][0][:, 2 + r * 16:2 + (r + 1) * 16, 256:257], 1.0), writes=(VBP[i][1 + r],))

        def kpart(kb):
            return 0 if kb < 2 else 1 + (kb - 2) // 16

        def kv_loads(hi):
            h = heads_l[hi]
            kt = kts[hi % 2][0]
            vb = vbs[hi % 2][0]
            qb, QB = qbs[hi % 2]
            KP, VP = KTP[hi % 2], VBP[hi % 2]
            for m in range(2):
                P.dma("sync", o_dma(qb[:, m, :], T["qt_b"][2 * h + m, :, :]), QB, True)
            for m in range(2):
                P.dma("sync", o_dma(kt[:, m, 0:CTX], T["kvc_k_b"][2 * h + m, :, :]), KP[0], True)
            P.dma("sync", o_dma(vb[:, 0:2, 0:256],
                                T["kvc_v_b"][:, h * 256:(h + 1) * 256].rearrange("(b p) e -> p b e", p=128)), VP[0], True)
            for r in range(4):
                for m in range(2):
                    c = 2 * h + m
                    P.dma("sync", o_dma(kt[:, m, CTX + r * TL:CTX + (r + 1) * TL],
                                        gat_rows(gat, r, R_KTB + c * 128, 128)), KP[1 + r], True)
                for i in range(TL // GCH):
                    P.dma("sync", o_dma(vb[:, 2 + r * 16 + 2 * i:4 + r * 16 + 2 * i, 0:256],
                                        gat_rows(gat, r, R_VB + i * GCH, GCH)[:, h * 256:(h + 1) * 256]
                                        .rearrange("(b p) e -> p b e", p=128)), VP[1 + r], True)

        deferred = []

        def tick(force=False):
            for d in list(deferred):
                d[0] -= 1
                if d[0] <= 0 or force:
                    deferred.remove(d)
                    d[1]()

        def finalize(h, qt, zb, ZB):
            q0 = qt * 256
            gbl = []
            for i in range(4):
                P.op("vector", o_copy(osb[i][0][:], pO[i][0][:, 0:257]), reads=(pO[i][1],), writes=(osb[i][1],))
            for qs in range(2):
                o1, O1 = osb[qs]
                o2, O2 = osb[2 + qs]
                gb, GB = gbs[cnt["g"] % 4]
                cnt["g"] += 1
                P.op("vector", o_recip(sm[:, 0:1], o1[:, 256:257]), reads=(O1,), writes=(SM,))
                P.op("vector", o_recip(sm[:, 1:2], o2[:, 256:257]), reads=(O2,), writes=(SM,))
                P.op("vector", o_tt(sm[:, 1:2], sm[:, 1:2], lam, ALU.mult), reads=(SM, LSC), writes=(SM,))
                P.op("vector", o_ts(tmp[:], o2[:, 0:256], sm[:, 1:2], ALU.mult), reads=(O2, SM), writes=(TMP,))
                P.op("vector", o_stt(ot[:], o1[:, 0:256], sm[:, 0:1], tmp[:], ALU.mult, ALU.subtract),
                     reads=(O1, SM, TMP), writes=(OT,))
                P.op("vector", o_tt(sq[:], ot[:], ot[:], ALU.mult), reads=(OT,), writes=(SQ,))
                P.op("vector", o_rsum(sm[:, 2:3], sq[:]), reads=(SQ,), writes=(SM,))
                P.op("vector", o_ts(sm[:, 3:4], sm[:, 2:3], 1.0 / 256, ALU.mult, EPS, ALU.add),
                     reads=(SM,), writes=(SM,))
                P.op("gpsimd", o_tt(sm[:, 4:5], sm[:, 3:4], mhalf[:], ALU.pow), reads=(SM, MHALF), writes=(SM,))
                P.op("vector", o_tt(gz[:], gsub[:], zb[:, qs, :], ALU.mult), reads=(GSUB, ZB), writes=(GZ,))
                P.op("vector", o_stt(gb[:], ot[:], sm[:, 4:5], gz[:], ALU.mult, ALU.mult),
                     reads=(OT, SM, GZ), writes=(GB,))
                gbl.append((gb, GB))

            def tail(gbl=gbl, h=h, q0=q0):
                for qs, (gb, GB) in enumerate(gbl):
                    for ch in range(2):
                        P.op("tensor", o_tr(pT[:, ch, qs * 128:(qs + 1) * 128], gb[:, ch * 128:(ch + 1) * 128], idb[:]),
                             reads=(GB, IDB), writes=(PT,), inc=(ch == 1))
                g2, G2 = gtb[cnt["t"] % 2]
                cnt["t"] += 1
                P.op("vector", o_copy(g2[:], pT[:]), reads=(PT,), writes=(G2,))
                P.dma("sync", o_dma(T["gt_b"][2 * h:2 * h + 2, :, q0:q0 + 256].rearrange("c p t -> p c t"), g2[:]), G2, False)
            deferred.append([24, tail])

        kv_loads(0)
        for hi, h in enumerate(heads_l):
            kt, KT = kts[hi % 2]
            vb, VB = vbs[hi % 2]
            qb, QB = qbs[hi % 2]
            seq = [(qt, ki, kb, nk) for qt in range(nqt)
                   for nk, kl in [(NKB, range(NKB)) if qt < 8 else (2, range(2))] for ki, kb in enumerate(kl)]

            def qk(item):
                qt, ki, kb, nk = item
                s_, S_ = pS[cnt["s"] % 3]
                cnt["s"] += 1
                for m in range(2):
                    P.op("tensor", o_mm(s_[:, m * 256:(m + 1) * 256], kt[:, m, kb * 128:(kb + 1) * 128],
                                        qb[:, m, qt * 256:(qt + 1) * 256], True, True), reads=(KTP[hi % 2][kpart(kb)], QB), writes=(S_,),
                         inc=(m == 1))
                return s_, S_
            ahead = [qk(seq[0]), qk(seq[1])]
            zcur = None
            for idx, (qt, ki, kb, nk) in enumerate(seq):
                if ki == 0:
                    zb, ZB = zbs[cnt["z"] % 2]
                    cnt["z"] += 1
                    q0 = qt * 256
                    P.dma("sync", o_dma(zb[:], T["z_b"][q0:q0 + 256, h * 256:(h + 1) * 256]
                                        .rearrange("(s p) e -> p s e", p=128)), ZB, True)
                    zcur = (zb, ZB)
                s_, S_ = ahead.pop(0)
                e_, E_ = es[cnt["e"] % 6]
                cnt["e"] += 1
                P.op("scalar", o_act(e_[:], s_[:], AF.Exp, scale=SCALE), reads=(S_,), writes=(E_,))
                if idx + 2 < len(seq):
                    ahead.append(qk(seq[idx + 2]))
                for m in range(2):
                    for qs in range(2):
                        o_, O_ = pO[m * 2 + qs]
                        P.op("tensor", o_mm(o_[:, 0:257], e_[:, m * 256 + qs * 128:m * 256 + (qs + 1) * 128],
                                            vb[:, kb, :], ki == 0, ki == nk - 1), reads=(E_, VBP[hi % 2][kpart(kb)]), writes=(O_,),
                             inc=(m == 1 and qs == 1))
                tick()
                if ki == nk - 1:
                    finalize(h, qt, zcur[0], zcur[1])
                    if qt == 0 and hi + 1 < len(heads_l):
                        kv_loads(hi + 1)
        tick(force=True)
    run_phase(K, body)


def phase_C(K, l, last, xsrc, xdst):
    T = K.T[l]
    passes = [(0, 1024), (1024, 1024)] + ([] if last else [(2048, 256)])
    for (p0, pn) in passes:
        hold = {}

        def step1(P, sb, ps, p0=p0, pn=pn):
            mT, MT = hold["mT"]
            gta, _ = sb("gta", [128, 16, pn], BF16)
            gtb, _ = sb("gtb", [128, 16, pn], BF16)
            GTAS = [Buf(f"gta{q}") for q in range(4)]
            GTBS = [Buf(f"gtb{q}") for q in range(4)]
            for q in range(4):
                P.dma("sync", o_dma(gta[:, 4 * q:4 * q + 4, :],
                                    T["gt_a"][4 * q:4 * q + 4, :, p0:p0 + pn].rearrange("c p t -> p c t")), GTAS[q], True)
            for q in range(4):
                P.dma("sync", o_dma(gtb[:, 4 * q:4 * q + 4, :],
                                    T["gt_b"][4 * q:4 * q + 4, :, p0:p0 + pn].rearrange("c p t -> p c t")), GTBS[q], True)
            wva = K.w_proj_a[l].rearrange("(k p) c -> p k c", p=128)
            wvb = K.w_proj_b[l].rearrange("(k p) c -> p k c", p=128)
            was = [sb("wpa", [128, 16, 512], BF16) for _ in range(2)]
            wbs = [sb("wpb", [128, 16, 512], BF16) for _ in range(2)]
            sgs = [sb("sg", [128, 2, 512], BF16) for _ in range(2)]
            t1s = [sb("mt1", [128, 512], F32) for _ in range(2)]
            t2s = [sb("mt2", [128, 512], F32) for _ in range(2)]
            pA = [ps("pA", [128, 512], F32) for _ in range(2)]
            pB = [ps("pB", [128, 512], F32) for _ in range(2)]
            n = 0
            tn = min(512, pn)
            def wab_load(g):
                if g < 4:
                    P.dma("gpsimd", o_dma(was[g % 2][0][:], wva[:, :, g * 512:(g + 1) * 512]), was[g % 2][1], True)
                    P.dma("gpsimd", o_dma(wbs[g % 2][0][:], wvb[:, :, g * 512:(g + 1) * 512]), wbs[g % 2][1], True)
            wab_load(0)
            for g in range(4):
                wa, WA = was[g % 2]
                wb, WB = wbs[g % 2]
                wab_load(g + 1)
                for cb in range(4):
                    c = g * 4 + cb
                    for tt in range(pn // tn):
                        ts = slice(tt * tn, (tt + 1) * tn)
                        ya, YA = pA[n % 2]
                        yb, YB = pB[n % 2]
                        sg, SG = sgs[n % 2]
                        t1, T1 = t1s[n % 2]
                        t2, T2 = t2s[n % 2]
                        n += 1
                        P.dma("sync", o_dma(sg[:, 0, 0:tn], T["sg_a"][c, :, p0 + tt * tn:p0 + (tt + 1) * tn]), SG, True)
                        P.dma("sync", o_dma(sg[:, 1, 0:tn], T["sg_b"][c, :, p0 + tt * tn:p0 + (tt + 1) * tn]), SG, True)
                        for kc in range(16):
                            P.op("tensor", o_mm(ya[:, 0:tn], wa[:, kc, cb * 128:(cb + 1) * 128], gta[:, kc, ts],
                                                kc == 0, kc == 15), reads=(WA, GTAS[kc // 4]), writes=(YA,), inc=(kc == 15))
                        for kc in range(16):
                            P.op("tensor", o_mm(yb[:, 0:tn], wb[:, kc, cb * 128:(cb + 1) * 128], gtb[:, kc, ts],
                                                kc == 0, kc == 15), reads=(WB, GTBS[kc // 4]), writes=(YB,), inc=(kc == 15))
                        P.op("vector", o_tt(t1[:, 0:tn], ya[:, 0:tn], sg[:, 0, 0:tn], ALU.mult), reads=(YA, SG), writes=(T1,))
                        P.op("vector", o_tt(t2[:, 0:tn], yb[:, 0:tn], sg[:, 1, 0:tn], ALU.mult), reads=(YB, SG), writes=(T2,))
                        P.op("gpsimd", o_tt(mT[:, c, ts], t1[:, 0:tn], t2[:, 0:tn], ALU.add), reads=(T1, T2), writes=(MT,))

        def step2(P, sb, ps, p0=p0, pn=pn):
            mT, MT = hold["mT"]
            r = 0 if p0 < TL else 1
            wvo = K.w_out[l].rearrange("(k p) c -> p k c", p=128)
            wo, _ = sb("wo", [128, 4, 16, 512], BF16)
            WOS = [Buf(f"wo{g}") for g in range(4)]
            for g in range(4):
                P.dma("gpsimd", o_dma(wo[:, g], wvo[:, :, g * 512:(g + 1) * 512]), WOS[g], True)
            gp, GP = sb("gp", [128, D], F32)
            gpo, GPO = sb("gpo", [128, D], F32)
            P.dma("sync", o_dma(gp[:], K.mod[l][r, 2 * D:3 * D].partition_broadcast(128)), GP, True)
            P.dma("sync", o_dma(gpo[:], K.g_post[l].partition_broadcast(128)), GPO, True)
            P.op("vector", o_tt(gp[:], gp[:], gpo[:], ALU.mult), reads=(GP, GPO), writes=(GP,))
            xts = [sb("xr", [128, D], F32) for _ in range(2)]
            ys = [sb("yo", [128, D], F32) for _ in range(2)]
            junk, JUNK = sb("junk", [128, 512], BF16)
            st_, ST = sb("stc", [128, 8], F32)
            pY = [ps("pY", [128, 512], F32) for _ in range(8)]
            for t in range(pn // 128):
                xt, XT = xts[t % 2]
                y, Y = ys[t % 2]
                tok = p0 + t * 128
                P.dma("sync", o_dma(xt[:], xsrc[tok:tok + 128, :]), XT, True)
                banks = [pY[(t % 2) * 4 + g] for g in range(4)]
                for g in range(4):
                    b_, B_ = banks[g]
                    for kc in range(16):
                        P.op("tensor", o_mm(b_[:], mT[:, kc, t * 128:(t + 1) * 128], wo[:, g, kc, :], kc == 0, kc == 15),
                             reads=(MT, WOS[g]), writes=(B_,), inc=(kc == 15))
                    P.op("scalar", o_act(junk[:], b_[:], AF.Square, accum_out=st_[:, g:g + 1]),
                         reads=(B_,), writes=(JUNK, ST))
                P.op("vector", o_rsum(st_[:, 4:5], st_[:, 0:4]), reads=(ST,), writes=(ST,))
                P.op("vector", o_ts(st_[:, 5:6], st_[:, 4:5], 1.0 / D, ALU.mult, EPS, ALU.add), reads=(ST,), writes=(ST,))
                P.op("scalar", o_act(st_[:, 5:6], st_[:, 5:6], AF.Ln), reads=(ST,), writes=(ST,))
                P.op("scalar", o_act(st_[:, 6:7], st_[:, 5:6], AF.Exp, scale=-0.5), reads=(ST,), writes=(ST,))
                for g in range(4):
                    b_, B_ = banks[g]
                    cs = slice(g * 512, (g + 1) * 512)
                    P.op("vector", o_stt(y[:, cs], b_[:], st_[:, 6:7], gp[:, cs], ALU.mult, ALU.mult),
                         reads=(B_, ST, GP), writes=(Y,))
                P.op("gpsimd", o_tt(y[:], y[:], xt[:], ALU.add), reads=(Y, XT), writes=(Y,))
                P.dma("sync", o_dma(xdst[tok:tok + 128, :], y[:]), Y, False)

        with ExitStack() as st0:
            K.pid += 1
            mt = st0.enter_context(K.nc.sbuf_tensor(f"mT_{K.pid}", [128, 16, pn], BF16))
            hold["mT"] = (mt, Buf("mT"))
            run_phase(K, step1)
            run_phase(K, step2)


SCR = {
    "gth": ([GROWS, 2048], BF16), "gat": ([NGCH, 4, GCH, 2048], BF16),
    "kvc_k_a": ([4, 128, CTX], BF16), "kvc_k_b": ([16, 128, CTX], BF16),
    "kvc_v_a": ([CTX, 512], BF16), "kvc_v_b": ([CTX, 2048], BF16),
    "qt_a": ([16, 128, NT], BF16), "qt_b": ([16, 128, NT], BF16),
    "z_a": ([NT, D], BF16), "z_b": ([NT, D], BF16),
    "sg_a": ([16, 128, NT], BF16), "sg_b": ([16, 128, NT], BF16),
    "gt_a": ([16, 128, NT], BF16), "gt_b": ([16, 128, NT], BF16),
}
WSHAPES = {"w_ada": [D, 3 * D], "b_ada": [3 * D], "g_pre": [D], "g_post": [D], "w_in": [D, IN_COLS],
           "sink": [16], "lam_qk": [4, 128], "g_subln": [256], "w_proj_a": [D, D], "w_proj_b": [D, D],
           "w_out": [D, D]}


def new_ctx():
    K = Ctx()
    K.nc = bass.Bass("TRN2", target_bir_lowering=False)
    K.pid = 0
    K.marks = []
    K.ecount = {e: 0 for e in CENGS}
    K.waited = {e: {} for e in ENGS}
    K.T = {0: {}, 1: {}}
    return K


def declare(K, name, shape, dt, kind):
    return K.nc.dram_tensor(name, list(shape), dt, kind=kind).ap()


def declare_weights(K, layers, names):
    for n in names:
        if not hasattr(K, n):
            setattr(K, n, {})
        for l in layers:
            getattr(K, n)[l] = declare(K, f"{n}{l}", WSHAPES[n], F32, "ExternalInput")


def build_program():
    K = new_ctx()
    K.mod = {}
    K.xall = declare(K, "xall", [NT, D], F32, "ExternalInput")
    K.cc = declare(K, "cc", [2, D], F32, "ExternalInput")
    K.rowbase = declare(K, "rowbase", [64, 1], F32, "ExternalInput")
    K.mvalid = declare(K, "mvalid", [10], F32, "ExternalInput")
    K.ropeC = declare(K, "ropeC", [128, TL], F32, "Internal")
    K.ropeS = declare(K, "ropeS", [128, TL], F32, "Internal")
    declare_weights(K, range(DEPTH), list(WSHAPES))
    for ll in range(DEPTH):
        K.mod[ll] = declare(K, f"mod{ll}", [2, 3 * D], F32, "Internal")
        for n in SCR:
            K.T[ll][n] = declare(K, f"{n}{ll}", SCR[n][0], SCR[n][1], "Internal")
    K.x1 = declare(K, "x1", [NT, D], F32, "Internal")
    K.xout = declare(K, "xout", [TL, D], F32, "ExternalOutput")
    with ExitStack() as st:
        K.esets = {ll: {e: st.enter_context(K.nc.semaphore(f"es{ll}_{e}")) for e in CENGS} for ll in range(DEPTH)}
        K.ccsem = {ll: st.enter_context(K.nc.semaphore(f"cc{ll}")) for ll in range(DEPTH)}
        K.dslots = [[st.enter_context(K.nc.semaphore(f"ds{i}")), 0] for i in range(40)]
        use_layer_sems(K, 0)
        phase_rope(K)
        for ll in range(DEPTH):
            phase_adaln(K, ll)
        for ll in range(DEPTH):
            lst = ll == DEPTH - 1
            if ll > 0:
                use_layer_sems(K, ll)
            xsrc = K.xall if ll == 0 else K.x1
            xdst = K.xout if lst else K.x1
            phase_A(K, ll, lst, xsrc, gather=True)
            phase_attnA(K, ll, lst)
            phase_attnB(K, ll, lst)
            phase_C(K, ll, lst, xsrc, xdst)
    return K.nc


_PROG = {}


def kernel(x, c, ctx, c_ctx, w_ada, b_ada, g_pre, g_post, w_in, sink, lam_qk, g_subln,
           w_proj_a, w_proj_b, w_out):
    f = lambda a: np.ascontiguousarray(np.asarray(a, dtype=np.float32))
    x, c, ctx, c_ctx = f(x), f(c), f(ctx), f(c_ctx)
    W = {"w_ada": f(w_ada), "b_ada": f(b_ada), "g_pre": f(g_pre), "g_post": f(g_post), "w_in": f(w_in),
         "sink": f(sink), "lam_qk": f(lam_qk), "g_subln": f(g_subln), "w_proj_a": f(w_proj_a),
         "w_proj_b": f(w_proj_b), "w_out": f(w_out)}
    cores = list(range(8))
    if "nc" not in _PROG:
        _PROG["nc"] = build_program()
    ins = []
    for r in cores:
        b, s = r // 4, r % 4
        rowbase = np.zeros((64, 1), np.float32)
        rowbase[:32] = s * (TL // 64)
        mvalid = np.zeros((10,), np.float32)
        for q in range(4):
            mvalid[q] = 1.0 if q == s - 1 else 0.0
            mvalid[4 + q] = 1.0 if q == s + 1 else 0.0
        mvalid[8:] = 1.0
        d = {"xall": np.concatenate([x[b, s * TL:(s + 1) * TL], ctx[b]], 0),
             "cc": np.stack([c[b], c_ctx], 0), "rowbase": rowbase, "mvalid": mvalid}
        for n in WSHAPES:
            for l in range(DEPTH):
                d[f"{n}{l}"] = W[n][l]
        ins.append(d)
    res = run_bass_kernel_spmd(_PROG["nc"], ins, core_ids=cores).results
    out = np.zeros((2, SEQ, D), np.float32)
    for r in cores:
        out[r // 4, (r % 4) * TL:(r % 4 + 1) * TL] = res[r]["xout"]
    return out
```
